# Optimizing a Trainium2 kernel written in Bass

```python
import math
import jax, jax.numpy as jnp
from jax import lax
import numpy as np

D_MODEL = 1024
BATCH = 32
SEQ = 256
DEPTH = 2
DEC_BATCH = 2
DEC_SEQ = 1024
PAST_LEN = 512

GRID_W = 64
HEAD_DIM = 64
ROPE_BASE = 10000.0
H_RET = 8
RET_CHUNK = 128
H_WIN = 8
KV_WIN = 2
G_WIN = H_WIN // KV_WIN
WINDOW = 128
Q_BLOCK = 128
H_DIFF = 6
FNET_GROUPS = 4
FNET_DIM = 64
D_FF = 256 * math.ceil(8 * D_MODEL / 3 / 256)
N_AB = (DEPTH + 1) // 2
N_CD = DEPTH // 2
AB_IN = 4 * H_RET * HEAD_DIM + (H_WIN + 2 * KV_WIN) * HEAD_DIM
AB_OUT = (H_RET + H_WIN) * HEAD_DIM
CD_IN = 3 * H_DIFF * 2 * HEAD_DIM + FNET_GROUPS * FNET_DIM
CD_OUT = H_DIFF * 2 * HEAD_DIM + FNET_GROUPS * FNET_DIM
ALPHA = (2 * DEPTH) ** 0.25
BETA = (8 * DEPTH) ** -0.25
LN_EPS = 1e-5

kernel_name = 'hybrid_retention_window_diff_fnet_dit_step'

f32 = jnp.float32


def layer_norm(x, g=None, b=None):
    xf = x.astype(f32)
    mu = jnp.mean(xf, -1, keepdims=True)
    var = jnp.mean(jnp.square(xf - mu), -1, keepdims=True)
    y = (xf - mu) * lax.rsqrt(var + LN_EPS)
    if g is not None:
        y = y * g.astype(f32) + b.astype(f32)
    return y.astype(x.dtype)


def rms_norm(x, g):
    xf = x.astype(f32)
    y = xf * lax.rsqrt(jnp.mean(jnp.square(xf), -1, keepdims=True) + LN_EPS) * g.astype(f32)
    return y.astype(x.dtype)


def axial_rope(n_tokens, dim):
    rows = n_tokens // GRID_W
    r, col = jnp.meshgrid(jnp.arange(rows), jnp.arange(GRID_W), indexing='ij')
    r = r.reshape(-1).astype(f32)
    col = col.reshape(-1).astype(f32)
    quarter = dim // 4
    inv = ROPE_BASE ** (-jnp.arange(quarter, dtype=f32) / quarter)
    ang = jnp.concatenate([r[:, None] * inv, col[:, None] * inv], -1)
    return jnp.cos(ang), jnp.sin(ang)


def apply_rope(x, cos, sin):
    half = x.shape[-1] // 2
    shape = (1, x.shape[1]) + (1,) * (x.ndim - 3) + (half,)
    cos = cos.reshape(shape)
    sin = sin.reshape(shape)
    xf = x.astype(f32)
    x1, x2 = xf[..., :half], xf[..., half:]
    return jnp.concatenate([x1 * cos - x2 * sin, x1 * sin + x2 * cos], -1).astype(x.dtype)


def map_query_blocks(fn, q):
    B, T = q.shape[:2]
    nb = T // Q_BLOCK

    def body(n):
        start = n * Q_BLOCK
        return fn(lax.dynamic_slice_in_dim(q, start, Q_BLOCK, axis=1), start)

    out = jnp.moveaxis(lax.map(body, jnp.arange(nb)), 0, 1)
    return out.reshape((B, T) + out.shape[3:])


def gqa_attend(q, k, v, mask, sink):
    s = jnp.einsum('bqhgd,bkhd->bhgqk', q, k).astype(f32) * (q.shape[-1] ** -0.5)
    if mask is not None:
        s = jnp.where(mask, s, -jnp.inf)
    sink_col = jnp.broadcast_to(sink.astype(f32)[None, :, :, None, None], s.shape[:-1] + (1,))
    p = jax.nn.softmax(jnp.concatenate([s, sink_col], -1), axis=-1)[..., :-1]
    return jnp.einsum('bhgqk,bkhd->bqhgd', p.astype(v.dtype), v)


def window_attention_latent(q, k, v, ctx_k, ctx_v, sink):
    T = q.shape[1]
    Tc = ctx_k.shape[1]
    pad = ((0, 0), (Q_BLOCK, Q_BLOCK), (0, 0), (0, 0))
    kp = jnp.pad(k, pad)
    vp = jnp.pad(v, pad)
    a = jnp.arange(Q_BLOCK)[:, None]
    b = jnp.arange(3 * Q_BLOCK)[None, :]
    band = jnp.abs(a - b + Q_BLOCK) <= WINDOW
    ctx_mask = jnp.ones((Q_BLOCK, Tc), bool)

    def block(qb, start):
        kb = lax.dynamic_slice_in_dim(kp, start, 3 * Q_BLOCK, axis=1)
        vb = lax.dynamic_slice_in_dim(vp, start, 3 * Q_BLOCK, axis=1)
        j = start - Q_BLOCK + b
        mask = jnp.concatenate([band & (j >= 0) & (j < T), ctx_mask], 1)
        return gqa_attend(qb, jnp.concatenate([kb, ctx_k], 1), jnp.concatenate([vb, ctx_v], 1), mask, sink)

    return map_query_blocks(block, q)


def retention_chunkwise(q, k, v, log_gamma, s0):
    B, H, T, _ = q.shape
    dv = v.shape[-1]
    n = T // RET_CHUNK

    def chunks(x):
        return jnp.moveaxis(x.reshape(B, H, n, RET_CHUNK, x.shape[-1]), 2, 0)

    idx = jnp.arange(RET_CHUNK, dtype=f32)
    lg = log_gamma[:, None]
    diff = idx[:, None] - idx[None, :]
    intra = jnp.where(diff >= 0, jnp.exp(lg[:, :, None] * jnp.maximum(diff, 0.0)), 0.0)
    q_decay = jnp.exp(lg * (idx + 1.0))
    k_decay = jnp.exp(lg * (RET_CHUNK - 1.0 - idx))
    chunk_decay = jnp.exp(lg * RET_CHUNK)[..., None]

    def step(s, qkv):
        qc, kc, vc = qkv
        att = jnp.einsum('bhid,bhjd->bhij', qc, kc) * intra
        o = jnp.einsum('bhij,bhjv->bhiv', att, vc) + jnp.einsum('bhid,bhdv->bhiv', qc, s) * q_decay[..., None]
        s = s * chunk_decay + jnp.einsum('bhjd,hj,bhjv->bhdv', kc, k_decay, vc)
        return s, o

    s_fin, o = lax.scan(step, s0, (chunks(q), chunks(k), chunks(v)))
    return jnp.moveaxis(o, 0, 2).reshape(B, H, T, dv), s_fin


def bidir_retention(q, k, v, log_gamma, s0):
    def flip(t):
        return jnp.flip(t, 2)

    o, s = jax.vmap(retention_chunkwise)(jnp.stack([q, flip(q)]), jnp.stack([k, flip(k)]),
                                         jnp.stack([v, flip(v)]), log_gamma, s0)
    return o[0] + flip(o[1]), s


def mixer_ab(u, w_in, w_out, log_gamma, gn_g, gn_b, sink, ctx):
    B, T, _ = u.shape
    hr = H_RET * HEAD_DIM
    sizes = [hr, hr, hr, hr, H_WIN * HEAD_DIM, KV_WIN * HEAD_DIM]
    rq, rk, rv, rg, wq, wk, wv = jnp.split(u @ w_in, np.cumsum(sizes).tolist(), axis=-1)
    rq = rq.reshape(B, T, H_RET, HEAD_DIM)
    rk = rk.reshape(B, T, H_RET, HEAD_DIM)
    rv = rv.reshape(B, T, H_RET, HEAD_DIM)
    wq = wq.reshape(B, T, KV_WIN, G_WIN, HEAD_DIM)
    wk = wk.reshape(B, T, KV_WIN, HEAD_DIM)
    wv = wv.reshape(B, T, KV_WIN, HEAD_DIM)
    if ctx is None:
        s0 = jnp.zeros((2, B, H_RET, HEAD_DIM, HEAD_DIM), f32)
        wo = map_query_blocks(lambda qb, start: gqa_attend(qb, wk, wv, None, sink), wq)
    else:
        state, ck, cv = ctx
        cos, sin = axial_rope(T, HEAD_DIM)
        rq, rk = apply_rope(rq, cos, sin), apply_rope(rk, cos, sin)
        wq, wk = apply_rope(wq, cos, sin), apply_rope(wk, cos, sin)
        s0 = jnp.moveaxis(state.astype(f32), 1, 0)
        wo = window_attention_latent(wq, wk, wv, ck, cv, sink)

    def to_bh(t):
        return jnp.transpose(t, (0, 2, 1, 3)).astype(f32)

    ro, s_fin = bidir_retention(to_bh(rq), to_bh(rk) * (HEAD_DIM ** -0.5), to_bh(rv), log_gamma.astype(f32), s0)
    ro = layer_norm(ro)
    ro = jnp.transpose(ro, (0, 2, 1, 3)).reshape(B, T, hr) * gn_g.astype(f32) + gn_b.astype(f32)
    ro = jax.nn.silu(rg) * ro.astype(u.dtype)
    out = jnp.concatenate([ro, wo.reshape(B, T, -1)], -1) @ w_out
    return out, (jnp.moveaxis(s_fin, 0, 1), wk, wv)


def mixer_cd(u, w_in, w_out, lam, subln_g, lam_init, ctx):
    B, T, _ = u.shape
    qd = H_DIFF * 2 * HEAD_DIM
    dq, dk, dv, fz = jnp.split(u @ w_in, [qd, 2 * qd, 3 * qd], axis=-1)
    dq = dq.reshape(B, T, H_DIFF, 2, HEAD_DIM)
    dk = dk.reshape(B, T, H_DIFF, 2, HEAD_DIM)
    dv = dv.reshape(B, T, H_DIFF, 2 * HEAD_DIM)
    lam = lam.astype(f32)
    lam_full = jnp.exp(jnp.sum(lam[0] * lam[1])) - jnp.exp(jnp.sum(lam[2] * lam[3])) + lam_init
    if ctx is None:
        keys, vals = dk, dv
    else:
        ck, cv = ctx
        cos, sin = axial_rope(T, HEAD_DIM)
        dq, dk = apply_rope(dq, cos, sin), apply_rope(dk, cos, sin)
        keys = jnp.concatenate([dk, ck], 1)
        vals = jnp.concatenate([dv, cv], 1)

    def block(qb, start):
        s = jnp.einsum('bqhmd,bkhmd->bhmqk', qb, keys).astype(f32) * (HEAD_DIM ** -0.5)
        p = jax.nn.softmax(s, axis=-1)
        p = p[:, :, 0] - lam_full * p[:, :, 1]
        return jnp.einsum('bhqk,bkhe->bqhe', p.astype(vals.dtype), vals)

    a = map_query_blocks(block, dq)
    a = rms_norm(a, subln_g) * (1.0 - lam_init)
    z = fz.reshape(B, T, FNET_GROUPS, FNET_DIM).astype(f32)
    z = jnp.real(jnp.fft.fft2(z, axes=(1, 3), norm='ortho')).astype(u.dtype)
    out = jnp.concatenate([a.reshape(B, T, -1), z.reshape(B, T, -1)], -1) @ w_out
    return out, (dk, dv)


def swiglu(u, w_gate, w_up, w_down):
    return (jax.nn.silu(u @ w_gate) * (u @ w_up)) @ w_down


def trunk(x, cond, ctx_ab, ctx_cd, w_mod, b_mod, ln_g, ln_b, w_in_ab, w_out_ab, ret_log_gamma, ret_gn_g,
          ret_gn_b, win_sink, w_in_cd, w_out_cd, diff_lambda, diff_subln_g, w_gate, w_up, w_down):
    new_ab, new_cd = [], []
    for l in range(DEPTH):
        mod = jax.nn.silu(cond) @ w_mod[l] + b_mod[l]
        sh1, sc1, g1, sh2, sc2, g2 = jnp.split(mod[:, None, :], 6, axis=-1)
        u = layer_norm(x) * (1.0 + sc1) + sh1
        i = l // 2
        if l % 2 == 0:
            ctx = None if ctx_ab is None else tuple(t[:, i] for t in ctx_ab)
            h, new = mixer_ab(u, w_in_ab[i], w_out_ab[i], ret_log_gamma[i], ret_gn_g[i], ret_gn_b[i],
                              win_sink[i].reshape(KV_WIN, G_WIN), ctx)
            new_ab.append(new)
        else:
            ctx = None if ctx_cd is None else tuple(t[:, i] for t in ctx_cd)
            lam_init = 0.8 - 0.6 * math.exp(-0.3 * l)
            h, new = mixer_cd(u, w_in_cd[i], w_out_cd[i], diff_lambda[i], diff_subln_g[i], lam_init, ctx)
            new_cd.append(new)
        x = layer_norm(ALPHA * x + g1 * h, ln_g[l, 0], ln_b[l, 0])
        u = layer_norm(x) * (1.0 + sc2) + sh2
        x = layer_norm(ALPHA * x + g2 * swiglu(u, w_gate[l], w_up[l], w_down[l]), ln_g[l, 1], ln_b[l, 1])
    return x, new_ab, new_cd


def setup_inputs(seed: int = 0) -> dict:
    key = jax.random.key(seed)
    ks = jax.random.split(key, 32)

    def nrm(k, shape, s):
        return jax.random.normal(k, shape, f32) * s

    base_lg = jnp.log1p(-(2.0 ** (-5.0 - jnp.arange(H_RET, dtype=f32))))
    return {
        'x_prompt': nrm(ks[0], (BATCH, SEQ, D_MODEL), 1.0),
        'x_sample': nrm(ks[1], (DEC_BATCH, DEC_SEQ, D_MODEL), 1.0),
        'state_ret': nrm(ks[2], (DEC_BATCH, N_AB, 2, H_RET, HEAD_DIM, HEAD_DIM), 1.0),
        'cache_win_k': nrm(ks[3], (DEC_BATCH, N_AB, PAST_LEN, KV_WIN, HEAD_DIM), 1.0),
        'cache_win_v': nrm(ks[4], (DEC_BATCH, N_AB, PAST_LEN, KV_WIN, HEAD_DIM), 1.0),
        'cache_diff_k': nrm(ks[5], (DEC_BATCH, N_CD, PAST_LEN, H_DIFF, 2, HEAD_DIM), 1.0),
        'cache_diff_v': nrm(ks[6], (DEC_BATCH, N_CD, PAST_LEN, H_DIFF, 2 * HEAD_DIM), 1.0),
        'c': nrm(ks[7], (DEC_BATCH, D_MODEL), 1.0),
        'c_ctx': nrm(ks[8], (D_MODEL,), 1.0),
        'w_mod': nrm(ks[9], (DEPTH, D_MODEL, 6 * D_MODEL), D_MODEL ** -0.5),
        'b_mod': nrm(ks[10], (DEPTH, 6 * D_MODEL), 0.02),
        'ln_g': 1.0 + nrm(ks[11], (DEPTH, 2, D_MODEL), 0.02),
        'ln_b': nrm(ks[12], (DEPTH, 2, D_MODEL), 0.02),
        'w_in_ab': nrm(ks[13], (N_AB, D_MODEL, AB_IN), D_MODEL ** -0.5),
        'w_out_ab': nrm(ks[14], (N_AB, AB_OUT, D_MODEL), BETA * AB_OUT ** -0.5),
        'ret_log_gamma': base_lg * (1.0 + nrm(ks[15], (N_AB, 2, H_RET), 0.1)),
        'ret_gn_g': 1.0 + nrm(ks[16], (N_AB, H_RET * HEAD_DIM), 0.02),
        'ret_gn_b': nrm(ks[17], (N_AB, H_RET * HEAD_DIM), 0.02),
        'win_sink': nrm(ks[18], (N_AB, H_WIN), 1.0),
        'w_in_cd': nrm(ks[19], (N_CD, D_MODEL, CD_IN), D_MODEL ** -0.5),
        'w_out_cd': nrm(ks[20], (N_CD, CD_OUT, D_MODEL), BETA * CD_OUT ** -0.5),
        'diff_lambda': nrm(ks[21], (N_CD, 4, HEAD_DIM), 0.1),
        'diff_subln_g': 1.0 + nrm(ks[22], (N_CD, 2 * HEAD_DIM), 0.02),
        'w_gate': nrm(ks[23], (DEPTH, D_MODEL, D_FF), D_MODEL ** -0.5),
        'w_up': nrm(ks[24], (DEPTH, D_MODEL, D_FF), D_MODEL ** -0.5),
        'w_down': nrm(ks[25], (DEPTH, D_FF, D_MODEL), BETA * D_FF ** -0.5),
    }


def reference(x_prompt, x_sample, state_ret, cache_win_k, cache_win_v, cache_diff_k, cache_diff_v, c, c_ctx,
              w_mod, b_mod, ln_g, ln_b, w_in_ab, w_out_ab, ret_log_gamma, ret_gn_g, ret_gn_b, win_sink,
              w_in_cd, w_out_cd, diff_lambda, diff_subln_g, w_gate, w_up, w_down):
    weights = (w_mod, b_mod, ln_g, ln_b, w_in_ab, w_out_ab, ret_log_gamma, ret_gn_g, ret_gn_b, win_sink,
               w_in_cd, w_out_cd, diff_lambda, diff_subln_g, w_gate, w_up, w_down)
    y_prompt, new_ab, new_cd = trunk(x_prompt, c_ctx[None, :], None, None, *weights)
    y_sample, _, _ = trunk(x_sample, c, (state_ret, cache_win_k, cache_win_v), (cache_diff_k, cache_diff_v), *weights)
    new_state_ret = jnp.stack([n[0] for n in new_ab], axis=1)
    new_cache_win_k = jnp.stack([n[1] for n in new_ab], axis=1)
    new_cache_win_v = jnp.stack([n[2] for n in new_ab], axis=1)
    new_cache_diff_k = jnp.stack([n[0] for n in new_cd], axis=1)
    new_cache_diff_v = jnp.stack([n[1] for n in new_cd], axis=1)
    return (y_prompt, y_sample, new_state_ret, new_cache_win_k, new_cache_win_v, new_cache_diff_k, new_cache_diff_v)
```

```python
import math
from contextlib import ExitStack

import numpy as np
import ml_dtypes
import concourse.bass as bass
import concourse.mybir as mybir
from concourse.bass_utils import run_bass_kernel_spmd

F32 = mybir.dt.float32
BF16 = mybir.dt.bfloat16
ALU = mybir.AluOpType
AF = mybir.ActivationFunctionType
AX = mybir.AxisListType

NDMASEM = 24
D = 1024
DFF = 2816
ALPHA = 4.0 ** 0.25
EPS = 1e-5
NCORES = 8


class StopBuild(Exception):
    pass


class Prog:
    ENGS = ("pe", "act", "dve", "pool", "sp")

    def __init__(self, nc):
        self.nc = nc
        self.ops = {e: [] for e in self.ENGS}
        self.cnt = {e: 0 for e in self.ENGS}
        self.last_w = {}
        self.readers = {}
        self.known = {e: {} for e in self.ENGS}
        self.dma_cnt = [0] * NDMASEM
        self.dma_rr = 0
        self.bases = {}
        self.marks = []
        self.multi_w = {}

    def _deps(self, eng, reads, writes, append=False):
        need = {}

        def add(tok, same_ok):
            if tok is None:
                return
            sname, val, teng = tok
            if teng == eng and not same_ok:
                return
            if need.get(sname, 0) < val:
                need[sname] = val

        for k in reads:
            add(self.last_w.get(k), same_ok=(eng in ("act", "dve", "pool")))
            for t in self.multi_w.get(k, {}).values():
                add(t, same_ok=True)
        for k in writes:
            add(self.last_w.get(k), same_ok=False)
            if not append:
                for t in self.multi_w.get(k, {}).values():
                    add(t, same_ok=False)
            for t in self.readers.get(k, {}).values():
                add(t, same_ok=False)
            for k2 in self.bases.get(k.split(":")[0], ()):
                if k2 != k:
                    add(self.last_w.get(k2), same_ok=False)
                    for t in self.readers.get(k2, {}).values():
                        add(t, same_ok=False)
                    for t in self.multi_w.get(k2, {}).values():
                        add(t, same_ok=False)
        waits = []
        kn = self.known[eng]
        for sname, val in need.items():
            if kn.get(sname, 0) >= val:
                continue
            kn[sname] = val
            waits.append((sname, val))
        return waits

    def _commit(self, tok, reads, writes, append=False):
        for k in list(reads) + list(writes):
            self.bases.setdefault(k.split(":")[0], set()).add(k)
        def put(dct, k):
            d2 = dct.setdefault(k, {})
            key2 = (tok[0], tok[2])
            old = d2.get(key2)
            if old is None or old[1] < tok[1]:
                d2[key2] = tok
        for k in reads:
            put(self.readers, k)
        for k in writes:
            if append:
                put(self.multi_w, k)
            else:
                self.last_w[k] = tok
                self.readers[k] = {}
                self.multi_w[k] = {}

    PSUM_BASES = ("PA0", "PA1", "PB0", "PB1", "B4", "B5", "B6", "PTB")

    def op(self, eng, fn, r=(), w=(), signal=True):
        pr = [k for k in r if k.split(":")[0] in self.PSUM_BASES]
        if pr:
            r = [k for k in r if k not in pr]
            w = list(w) + pr
        waits = self._deps(eng, r, w)
        if signal:
            self.cnt[eng] += 1
            tok = ("c_" + eng, self.cnt[eng], eng)
            self.ops[eng].append((waits, fn, ("c_" + eng, 1)))
        else:
            tok = ("c_" + eng, self.cnt[eng] + 1, eng)
            self.ops[eng].append((waits, fn, None))
        self._commit(tok, r, w)

    def dma(self, q, fn, r=(), w=(), append=False):
        waits = self._deps(q, r, w, append)
        si = self.dma_rr % NDMASEM
        self.dma_rr += 1
        sname = "d%d" % si
        prev = self.dma_cnt[si]
        if prev > 0 and self.known[q].get(sname, 0) < prev:
            self.known[q][sname] = prev
            waits.append((sname, prev))
        self.dma_cnt[si] = prev + 16
        tok = (sname, prev + 16, "dma")
        self.ops[q].append((waits, fn, (sname, 16)))
        self._commit(tok, r, w, append)

    def emit(self):
        nc = self.nc
        with ExitStack() as st:
            sems = {}
            for e in ("pe", "act", "dve", "pool"):
                sems["c_" + e] = st.enter_context(nc.semaphore("c_" + e))
            for i in range(NDMASEM):
                sems["d%d" % i] = st.enter_context(nc.semaphore("d%d" % i))
            fin = []
            for e in ("pe", "act", "dve", "pool"):
                if self.cnt[e]:
                    fin.append(("c_" + e, self.cnt[e]))
            for i in range(NDMASEM):
                if self.dma_cnt[i]:
                    fin.append(("d%d" % i, self.dma_cnt[i]))
            block = st.enter_context(nc.Block())

            def run(name, eng):
                for waits, fn, inc in self.ops[name]:
                    for sname, val in waits:
                        eng.wait_ge(sems[sname], val)
                    ins = fn(eng)
                    if inc is not None:
                        ins.then_inc(sems[inc[0]], inc[1])

            @block.tensor
            def _(e):
                run("pe", e)

            @block.scalar
            def _(e):
                run("act", e)

            @block.vector
            def _(e):
                run("dve", e)

            @block.gpsimd
            def _(e):
                run("pool", e)

            @block.sync
            def _(e):
                run("sp", e)
                for sname, val in fin:
                    e.wait_ge(sems[sname], val)


def host_constants():
    c = {}
    t = np.arange(1024)
    row = (t // 64).astype(np.float64)
    col = (t % 64).astype(np.float64)
    inv = 10000.0 ** (-np.arange(16, dtype=np.float64) / 16)
    ang = np.concatenate([row[:, None] * inv, col[:, None] * inv], -1)
    cosT = np.zeros((128, 1024), np.float32)
    ssinT = np.zeros((128, 1024), np.float32)
    for r in range(128):
        d = r % 64
        j = d % 32
        cosT[r] = np.cos(ang[:, j])
        ssinT[r] = (-1.0 if d < 32 else 1.0) * np.sin(ang[:, j])
    c["cosT"] = cosT
    c["ssinT"] = ssinT
    i = np.arange(128, dtype=np.float64)
    ftab = np.zeros((128, 2, 128), np.float32)
    ftab[:, 0, :] = (i + 1)[None, :]
    ftab[:, 1, :] = (128 - i)[None, :]
    c["ftab"] = ftab
    etab = np.zeros((128, 2), np.float32)
    etab[:, 0] = 127 - i
    etab[:, 1] = i
    c["etab"] = etab
    jj = i[:, None]
    ii = i[None, :]
    c["rp"] = np.maximum(ii - jj, 0).astype(np.float32)
    c["rn"] = np.maximum(jj - ii, 0).astype(np.float32)
    c["ml"] = ((ii >= jj) * 0.125).astype(np.float32)
    c["mu"] = ((jj >= ii) * 0.125).astype(np.float32)
    c["mprev"] = np.where(jj >= ii, 0.0, -30000.0).astype(np.float32).astype(ml_dtypes.bfloat16)
    c["mnext"] = np.where(jj <= ii, 0.0, -30000.0).astype(np.float32).astype(ml_dtypes.bfloat16)
    f = np.arange(64, dtype=np.float64)
    a64 = 2 * np.pi * np.outer(f, f) / 64
    bdc = np.zeros((128, 128)); bds = np.zeros((128, 128))
    for g in range(2):
        bdc[g * 64:(g + 1) * 64, g * 64:(g + 1) * 64] = np.cos(a64)
        bds[g * 64:(g + 1) * 64, g * 64:(g + 1) * 64] = np.sin(a64)
    c["bdc"] = bdc.astype(np.float32).astype(ml_dtypes.bfloat16)
    c["bds"] = bds.astype(np.float32).astype(ml_dtypes.bfloat16)
    for T in (256, 1024):
        tt = np.arange(T, dtype=np.float64)
        a = 2 * np.pi * np.outer(tt, tt) / T
        nrm = 1.0 / math.sqrt(T * 64)
        C = (np.cos(a) * nrm).reshape(T // 128, 128, T).transpose(1, 0, 2)
        S = (-np.sin(a) * nrm).reshape(T // 128, 128, T).transpose(1, 0, 2)
        c["dftc%d" % T] = np.ascontiguousarray(C).astype(np.float32).astype(ml_dtypes.bfloat16)
        c["dfts%d" % T] = np.ascontiguousarray(S).astype(np.float32).astype(ml_dtypes.bfloat16)
    return c


CONST_SPECS = [
    ("cosT", [128, 1024], F32), ("ssinT", [128, 1024], F32), ("ftab", [128, 2, 128], F32),
    ("etab", [128, 2], F32), ("rp", [128, 128], F32), ("rn", [128, 128], F32), ("ml", [128, 128], F32),
    ("mu", [128, 128], F32), ("mprev", [128, 128], BF16), ("mnext", [128, 128], BF16),
    ("bdc", [128, 128], BF16), ("bds", [128, 128], BF16),
    ("dftc256", [128, 2, 256], BF16), ("dfts256", [128, 2, 256], BF16),
    ("dftc1024", [128, 8, 1024], BF16), ("dfts1024", [128, 8, 1024], BF16),
]

IN_SPECS = [
    ("xin", [2048, 1024]), ("cond", [2, 1024]), ("st_ret", [2, 8, 64, 64]), ("cwk", [512, 128]),
    ("cwv", [512, 128]), ("cdk", [512, 768]), ("cdv", [512, 768]),
    ("w_mod", [2, 1024, 6144]), ("b_mod", [2, 6144]), ("ln_g", [2, 2, 1024]), ("ln_b", [2, 2, 1024]),
    ("w_in_ab", [1024, 2816]), ("w_out_ab", [1024, 1024]), ("lgam", [16]), ("gn_g", [512]), ("gn_b", [512]),
    ("sink", [8]), ("w_in_cd", [1024, 2560]), ("w_out_cd", [1024, 1024]), ("dlam", [4, 64]), ("subln", [128]),
    ("w_gate", [2, 1024, 2816]), ("w_up", [2, 1024, 2816]), ("w_down", [2, 2816, 1024]),
]
OUT_SPECS = [
    ("y", [2048, 1024]), ("nstate", [4, 2, 8, 64, 64]), ("nwk", [4, 256, 128]), ("nwv", [4, 256, 128]),
    ("ndk", [4, 256, 768]), ("ndv", [4, 256, 768]),
]


def build_nc(stop="full"):
    nc = bass.Bass("TRN2", target_bir_lowering=False)
    I = {}
    for name, shape in IN_SPECS:
        I[name] = nc.dram_tensor(name, shape, F32, kind="ExternalInput").ap()
    for name, shape, dt in CONST_SPECS:
        I[name] = nc.dram_tensor(name, shape, dt, kind="ExternalInput").ap()
    O = {}
    for name, shape in OUT_SPECS:
        O[name] = nc.dram_tensor(name, shape, F32, kind="ExternalOutput").ap()
    P = Prog(nc)
    st = ExitStack()
    with st:
        def sb(name, shape, dt=F32):
            return st.enter_context(nc.sbuf_tensor(name, shape, dt))

        def ps(name, shape, dt=F32):
            return st.enter_context(nc.psum_tensor(name, shape, dt))

        X = sb("X", [128, 8, 1024])
        UT = sb("UT", [128, 8, 1024], BF16)
        OT = sb("OT", [128, 8, 1024], BF16)
        WOUT = sb("WOUT", [128, 8, 1024], BF16)
        WG = [sb("WG%d" % i, [128, 8, 768], BF16) for i in range(2)]
        WDr = [sb("WD%d" % i, [128, 2, 1024], BF16) for i in range(2)]
        WDS = [sb("WDS%d" % i, [128, 2, 1024], BF16) for i in range(2)]
        BC = [sb("BC%d" % i, [128, 1024]) for i in range(3)]
        COS = sb("COS", [128, 1024])
        SSIN = sb("SSIN", [128, 1024])
        ident = sb("ident", [128, 128], BF16)
        mhalf = sb("mhalf", [128, 16])
        FT = sb("FT", [128, 2, 128]); ET = sb("ET", [128, 2])
        MPV = sb("MPV", [128, 128], BF16); MNX = sb("MNX", [128, 128], BF16)
        BDC = sb("BDC", [128, 128], BF16); BDS = sb("BDS", [128, 128], BF16)
        DF256 = sb("DF256", [128, 2, 2, 256], BF16)
        LGB = sb("LGB", [128, 16]); LGC = sb("LGC", [128, 4, 2])
        DLALL = sb("DLALL", [128, 8, 128], BF16)
        KDEC = sb("KDEC", [128, 2, 8, 64], BF16)
        QD = sb("QD", [128, 2, 128]); CDt = sb("CDt", [128, 2])
        GNG = sb("GNG", [128, 512]); GNB = sb("GNB", [128, 512])
        ES = sb("ES", [128, 8])
        LAM = sb("LAM", [128, 4]); SUB = sb("SUB", [128, 128])
        CONDT = sb("CONDT", [128, 2, 8]); SCB = sb("SCB", [128, 8, 2], BF16); SBC = sb("SBC", [128, 2, 8, 128], BF16)
        MODT = sb("MODT", [128, 2, 2, 4, 8])
        GSCR = nc.dram_tensor("gscr", [2, 2, 128, 1024], F32).ap()
        BMT = sb("BMT", [128, 8])
        BMT0 = sb("BMT0", [128, 2, 8])
        NG = 11
        G = [sb("G%d" % i, [128, 1024]) for i in range(NG)]
        sm = sb("sm", [128, 64])
        sm2 = sm
        RPt = G[1][:, 0:128]; RNt = G[1][:, 128:256]; MLt = G[1][:, 256:384]; MUt = G[1][:, 384:512]
        DLM = G[1][:, 512:768].rearrange("p (a b) -> p a b", a=4)
        S32A = sb("S32A", [128, 4, 2, 128])
        LNS = sb("LNS", [128, 8, 16])
        PA = ps("PA", [128, 1024]); PB = ps("PB", [128, 1024])
        B4 = ps("B4", [128, 512]); B5 = ps("B5", [128, 512]); B6 = ps("B6", [128, 512])
        PTB = ps("PTB", [128, 1024], BF16)

        def gb(i, dt=F32):
            a = G[i][:]
            return a if dt == F32 else a.bitcast(BF16)

        def dma_sp(out, in_, r=(), w=(), slow=False, append=False):
            P.dma("sp", lambda e: e.dma_start(out=out, in_=in_, allow_slow_non_contiguous=slow), r=r, w=w, append=append)

        def dma_cast(out, in_, r=(), w=(), append=False):
            P.dma("pool", lambda e: e.dma_start(out=out, in_=in_), r=r, w=w, append=append)

        def mm(out, lhsT, rhs, start, stop, r, w):
            P.op("pe", lambda e: e.matmul(out, lhsT=lhsT, rhs=rhs, start=start, stop=stop), r=r, w=w, signal=bool(stop))

        def tr(out, in_, r, w):
            P.op("pe", lambda e: e.transpose(out=out, in_=in_, identity=ident[:]), r=list(r) + ["ident"], w=w)

        def act(out, in_, func, r, w, scale=1.0, bias=0.0, accum=None):
            if accum is None:
                P.op("act", lambda e: e.activation(out=out, in_=in_, func=func, scale=scale, bias=bias), r=r, w=w)
            else:
                P.op("act", lambda e: e.activation(out=out, in_=in_, func=func, scale=scale, bias=bias,
                                                   accum_out=accum), r=r, w=w)

        def tt(eng, out, in0, in1, op, r, w):
            P.op(eng, lambda e: e.tensor_tensor(out=out, in0=in0, in1=in1, op=op), r=r, w=w)

        def ts(eng, out, in0, s1, s2, op0, op1, r, w):
            if s2 is None:
                P.op(eng, lambda e: e.tensor_scalar(out=out, in0=in0, scalar1=s1, scalar2=None, op0=op0), r=r, w=w)
            else:
                P.op(eng, lambda e: e.tensor_scalar(out=out, in0=in0, scalar1=s1, scalar2=s2, op0=op0, op1=op1),
                     r=r, w=w)

        def stt(out, in0, scalar, in1, op0, op1, r, w):
            P.op("dve", lambda e: e.scalar_tensor_tensor(out=out, in0=in0, scalar=scalar, in1=in1, op0=op0, op1=op1),
                 r=r, w=w)

        def cp(eng, out, in_, r, w):
            if eng == "act":
                P.op("act", lambda e: e.copy(out=out, in_=in_), r=r, w=w)
            else:
                P.op(eng, lambda e: e.tensor_copy(out=out, in_=in_), r=r, w=w)

        def red(out, in_, r, w):
            P.op("dve", lambda e: e.tensor_reduce(out=out, in_=in_, axis=AX.X, op=ALU.add), r=r, w=w)

        def rsqrt_pool(ap, n, key):
            tt("pool", ap, ap, mhalf[:, 0:n], ALU.pow, r=[key, "mhalf"], w=[key])

        OTKEYS = ["OT%d" % k for k in range(8)]
        mod_bufs = [(WOUT, ["WOUT"]), (OT, OTKEYS)]
        mod_loaded = {}

        def mod_load(l, half, vi, bi):
            if (l, half, vi) in mod_loaded:
                return
            buf, keys = mod_bufs[bi]
            idx = 3 * half + vi
            dma_cast(buf[:], I["w_mod"][l][:, idx * 1024:(idx + 1) * 1024].rearrange("(k p) n -> p k n", p=128), w=keys)
            mod_loaded[(l, half, vi)] = (buf, keys)

        for c in range(2):
            dma_sp(CONDT[:, c, :], I["cond"][c].rearrange("(k p) -> p k", p=128), w=["CONDT"], slow=True)
        mod_load(0, 0, 0, 0)
        mod_load(0, 0, 1, 1)
        for vi_ in range(2):
            dma_sp(BMT0[:, vi_, :], I["b_mod"][0, vi_ * 1024:(vi_ + 1) * 1024].rearrange("(k p) -> p k", p=128),
                   w=["BMT0"], slow=True)
        act(SCB[:].rearrange("p k c -> p c k"), CONDT[:], AF.Silu, r=["CONDT"], w=["SCB"])
        P.op("pool", lambda e: e.memset(gb(2, BF16)[:, 0:2048], 0.0), w=["G2"])
        P.op("pool", lambda e: e.memset(ident[:], 0.0), w=["ident"])
        P.op("pool", lambda e: e.affine_select(out=ident[:], in_=ident[:], compare_op=ALU.not_equal, fill=1.0,
                                              base=0, pattern=[[-1, 128]], channel_multiplier=1),
             r=["ident"], w=["ident"])
        P.op("pool", lambda e: e.memset(mhalf[:], -0.5), w=["mhalf"])
        def init_tables():
            dma_sp(LGB[:], I["lgam"].partition_broadcast(128), w=["LGB"])
            for ap_, name in ((RPt, "rp"), (RNt, "rn"), (MLt, "ml"), (MUt, "mu")):
                dma_sp(ap_, I[name], w=["G1:" + name])
            for tile_, name in ((FT, "ftab"), (ET, "etab")):
                dma_sp(tile_[:], I[name], w=[name])
            for hh in range(2):
                for d in range(2):
                    dma_sp(LGC[hh * 64:(hh + 1) * 64, :, d],
                           I["lgam"][d * 8 + hh:d * 8 + 8:2].partition_broadcast(64), w=["LGC"], slow=True)
            dma_sp(GNG[:], I["gn_g"].partition_broadcast(128), w=["GNG"])
            dma_sp(GNB[:], I["gn_b"].partition_broadcast(128), w=["GNB"])
            for tile_, name in ((COS, "cosT"), (SSIN, "ssinT"), (MPV, "mprev"), (MNX, "mnext"), (BDC, "bdc"), (BDS, "bds")):
                dma_sp(tile_[:], I[name], w=[name])
            dma_sp(DF256[:, 0], I["dftc256"], w=["DF256"])
            dma_sp(DF256[:, 1], I["dfts256"], w=["DF256"])
            dma_sp(ES[:], I["sink"].partition_broadcast(128), w=["ES"])
            dma_sp(DLM[:].rearrange("p a b -> p (a b)"), I["dlam"].rearrange("a b -> (a b)").partition_broadcast(128),
                   w=["G1:DLM"])
            dma_sp(SUB[:], I["subln"].partition_broadcast(128), w=["SUB"])
            for h in range(8):
                e1 = gb(0)[:, 0:128]; e2 = gb(0)[:, 128:256]
                act(e1, RPt[:], AF.Exp, r=["G1:rp", "LGB"], w=["G0"], scale=LGB[:, h:h + 1])
                act(e2, RNt[:], AF.Exp, r=["G1:rn", "LGB"], w=["G0"], scale=LGB[:, 8 + h:9 + h])
                tt("dve", e1, e1, MLt[:], ALU.mult, r=["G0", "G1:ml"], w=["G0"])
                tt("dve", e2, e2, MUt[:], ALU.mult, r=["G0", "G1:mu"], w=["G0"])
                tt("dve", DLALL[:, h, :], e1, e2, ALU.add, r=["G0"], w=["DLALL"])
            for d in range(2):
                ts("dve", sm[:, d * 8:(d + 1) * 8], LGB[:, d * 8:(d + 1) * 8], ET[:, d:d + 1], None, ALU.mult, None,
                   r=["LGB", "etab"], w=["sm"])
            act(sm[:, 0:16], sm[:, 0:16], AF.Exp, r=["sm"], w=["sm"])
            ts("dve", KDEC[:].rearrange("p d h f -> p (d h) f"), sm[:, 0:16].unsqueeze(2).broadcast_to([128, 16, 64]),
               0.125, None, ALU.mult, None, r=["sm"], w=["KDEC"])
            act(ES[:], ES[:], AF.Exp, r=["ES"], w=["ES"])
            tt("dve", gb(0)[:, 0:64], DLM[:, 0, :], DLM[:, 1, :], ALU.mult, r=["G1:DLM"], w=["G0"])
            tt("dve", gb(0)[:, 64:128], DLM[:, 2, :], DLM[:, 3, :], ALU.mult, r=["G1:DLM"], w=["G0"])
            red(LAM[:, 0:2], gb(0)[:, 0:128].rearrange("p (a b) -> p a b", a=2), r=["G0"], w=["LAM"])
            act(LAM[:, 0:2], LAM[:, 0:2], AF.Exp, r=["LAM"], w=["LAM"])
            lam_init = 0.8 - 0.6 * math.exp(-0.3 * 1)
            tt("dve", LAM[:, 2:3], LAM[:, 1:2], LAM[:, 0:1], ALU.subtract, r=["LAM"], w=["LAM"])
            ts("dve", LAM[:, 2:3], LAM[:, 2:3], -lam_init, None, ALU.add, None, r=["LAM"], w=["LAM"])
            ts("dve", SUB[:], SUB[:], 1.0 - lam_init, None, ALU.mult, None, r=["SUB"], w=["SUB"])


        def ln_stats(src, rkeys):
            for hh in range(2):
                P.op("dve", lambda e, hh=hh: e.bn_stats(out=sm2[:, hh * 6:(hh + 1) * 6],
                                                        in_=src[:, hh * 512:(hh + 1) * 512]),
                     r=rkeys, w=["sm2a"])
            P.op("dve", lambda e: e.bn_aggr(out=sm2[:, 16:18], in_=sm2[:, 0:12]), r=["sm2a"], w=["sm2b"])
            ts("dve", sm2[:, 18:19], sm2[:, 17:18], EPS, None, ALU.add, None, r=["sm2b"], w=["sm2c"])
            rsqrt_pool(sm2[:, 18:19], 1, "sm2c")
            return sm2[:, 16:17], sm2[:, 18:19]

        def compute_mod(l, half, phase):
            if phase == 1:
                dma_sp(BC[0][:], GSCR[l, half], r=["GSCR%d%d" % (l, half)], w=["BC0"])
                dma_sp(BC[1][:], I["ln_g"][l, half].partition_broadcast(128), w=["BC1"])
                dma_sp(BC[2][:], I["ln_b"][l, half].partition_broadcast(128), w=["BC2"])
                return
            if l == 0 and half == 0:
                for c in range(2):
                    cp("dve", SBC[:, c], SCB[:, :, c].unsqueeze(2).broadcast_to([128, 8, 128]), r=["SCB"], w=["SBC"])
            mod_load(l, half, 0, 0)
            mod_load(l, half, 1, 1)
            for vi in range(3):
                idx = 3 * half + vi
                if vi == 2:
                    mod_load(l, half, 2, 0)
                WB, wbk = mod_loaded[(l, half, vi)]
                if vi < 2:
                    if l == 0 and half == 0:
                        bmt, bmk = BMT0[:, vi, :], "BMT0"
                    else:
                        bmt, bmk = BMT[:], "BMT"
                        dma_sp(BMT[:], I["b_mod"][l, idx * 1024:(idx + 1) * 1024].rearrange("(k p) -> p k", p=128),
                               w=["BMT"], slow=True)
                    for j in range(8):
                        for k in range(8):
                            mm(B4[:, 2 * j:2 * j + 2], WB[:, k, j * 128:(j + 1) * 128], SCB[:, k, :], k == 0, k == 7,
                               r=wbk + ["SCB"], w=["B4"])
                    for c in range(2):
                        if vi == 0:
                            tt("dve", MODT[:, c, l, 2 * half + vi, :], B4[:, c:16:2], bmt, ALU.add, r=["B4", bmk],
                               w=["MODT"])
                        else:
                            stt(MODT[:, c, l, 2 * half + vi, :], B4[:, c:16:2], 1.0, bmt, ALU.add, ALU.add,
                                r=["B4", bmk], w=["MODT"])
                else:
                    dma_sp(BC[0][:], I["b_mod"][l, idx * 1024:(idx + 1) * 1024].partition_broadcast(128), w=["BC0"])
                    g1 = gb(0)
                    for c in range(2):
                        pp = PA if c == 0 else PB
                        pk = "PA" if c == 0 else "PB"
                        for hf in range(2):
                            for k in range(8):
                                mm(pp[:, hf * 512:(hf + 1) * 512], SBC[:, c, k, :], WB[:, k, hf * 512:(hf + 1) * 512],
                                   k == 0, k == 7, r=["SBC"] + wbk, w=["%s%d" % (pk, hf)])
                    for hf in range(2):
                        tt("dve", g1[:, hf * 512:(hf + 1) * 512], BC[0][:, hf * 512:(hf + 1) * 512],
                           PB[:, hf * 512:(hf + 1) * 512], ALU.add, r=["BC0", "PB%d" % hf], w=["G0"])
                    dma_sp(GSCR[l, half], g1, r=["G0"], w=["GSCR%d%d" % (l, half)])
                    for hf in range(2):
                        tt("dve", BC[0][:, hf * 512:(hf + 1) * 512], BC[0][:, hf * 512:(hf + 1) * 512],
                           PA[:, hf * 512:(hf + 1) * 512], ALU.add, r=["BC0", "PA%d" % hf], w=["BC0"])
            dma_sp(BC[1][:], I["ln_g"][l, half].partition_broadcast(128), w=["BC1"])
            dma_sp(BC[2][:], I["ln_b"][l, half].partition_broadcast(128), w=["BC2"])

        def ln_stats_all():
            junk = gb(9, BF16)[:, 0:1024]
            for t in range(4, 8):
                act(junk, X[:, t, :], AF.Identity, r=["X%d" % t], w=["G9", "sm"], accum=sm[:, t - 4:t - 3])
                act(junk, X[:, t, :], AF.Square, r=["X%d" % t], w=["G9", "sm"], accum=sm[:, t:t + 1])
            for t in range(4):
                for hh in range(2):
                    P.op("dve", lambda e, t=t, hh=hh: e.bn_stats(out=LNS[:, t, hh * 6:(hh + 1) * 6],
                                                                 in_=X[:, t, hh * 512:(hh + 1) * 512]),
                         r=["X%d" % t], w=["LNS%d" % t])
                P.op("dve", lambda e, t=t: e.bn_aggr(out=LNS[:, t, 12:14], in_=LNS[:, t, 0:12]), r=["LNS%d" % t],
                     w=["LNS%d" % t])
            ts("dve", LNS[:, 4:8, 12], sm[:, 0:4], 1.0 / 1024, None, ALU.mult, None, r=["sm"], w=["LNS%d" % t for t in range(4, 8)])
            tt("dve", sm[:, 8:12], LNS[:, 4:8, 12], LNS[:, 4:8, 12], ALU.mult, r=["LNS%d" % t for t in range(4, 8)], w=["sm"])
            stt(LNS[:, 4:8, 13], sm[:, 4:8], 1.0 / 1024, sm[:, 8:12], ALU.mult, ALU.subtract, r=["sm"],
                w=["LNS%d" % t for t in range(4, 8)])
            ts("dve", LNS[:, :, 14], LNS[:, :, 13], EPS, None, ALU.add, None, r=["LNS%d" % t for t in range(8)], w=["LNSr"])
            tt("pool", LNS[:, :, 14], LNS[:, :, 14], mhalf[:, 0:8], ALU.pow, r=["LNSr", "mhalf"], w=["LNSr"])

        def ln_to_UT(half, c, l):
            ln_stats_all()
            for t in range(8):
                uh = gb(t % 2, BF16)[:, 0:1024]
                uk = "G%d" % (t % 2)
                ts("dve", uh, X[:, t, :], LNS[:, t, 12:13], LNS[:, t, 14:15], ALU.subtract, ALU.mult,
                   r=["X%d" % t, "LNS%d" % t, "LNSr"], w=[uk])
                for k in range(8):
                    tr(PTB[:, k * 128:(k + 1) * 128], uh[:, k * 128:(k + 1) * 128], r=[uk], w=["PTB"])
                for k in range(4):
                    act(UT[:, k, t * 128:(t + 1) * 128], PTB[:, k * 128:(k + 1) * 128], AF.Identity,
                        r=["PTB", "MODT"], w=["UT%d" % t], scale=MODT[:, c, l, 2 * half + 1, k:k + 1],
                        bias=MODT[:, c, l, 2 * half, k:k + 1])
                for k in range(4, 8):
                    ts("dve", UT[:, k, t * 128:(t + 1) * 128], PTB[:, k * 128:(k + 1) * 128],
                       MODT[:, c, l, 2 * half + 1, k:k + 1], MODT[:, c, l, 2 * half, k:k + 1], ALU.mult, ALU.add,
                       r=["PTB", "MODT"], w=["UT%d" % t])

        def post_ln_all():
            ln_stats_all()
            for t in range(8):
                stt(X[:, t, :], X[:, t, :], LNS[:, t, 12:13], BC[1][:], ALU.subtract, ALU.mult,
                    r=["X%d" % t, "LNS%d" % t, "BC1"], w=["X%d" % t])
                stt(X[:, t, :], X[:, t, :], LNS[:, t, 14:15], BC[2][:], ALU.mult, ALU.add,
                    r=["X%d" % t, "LNSr", "BC2"], w=["X%d" % t])

        def proj_fm(dst, wap_fn, wkey, ntok, dkey, psel=0, rope=None):
            nh = ntok // 512
            for hf in range(nh):
                pp = PA if (hf + psel) % 2 == 0 else PB
                pk = "PA" if (hf + psel) % 2 == 0 else "PB"
                ukeys = ["UT%d" % t for t in range(hf * 4, hf * 4 + 4)]
                for k in range(8):
                    mm(pp[:, 0:512], wap_fn(k), UT[:, k, hf * 512:(hf + 1) * 512], k == 0, k == 7,
                       r=[wkey] + ukeys, w=[pk + "0"])
                if rope is None:
                    cp("act", dst[:, hf * 512:(hf + 1) * 512], pp[:, 0:512], r=[pk + "0"], w=[dkey])
                else:
                    for k in range(8):
                        mm(pp[:, 512:1024], rope(k), UT[:, k, hf * 512:(hf + 1) * 512], k == 0, k == 7,
                           r=[wkey] + ukeys, w=[pk + "1"])
                    t1 = gb(1)[:, 0:512]; t2 = gb(1)[:, 512:1024]
                    tt("dve", t1, pp[:, 0:512], COS[:, hf * 512:(hf + 1) * 512], ALU.mult, r=[pk + "0", "cosT"], w=["G1:a"])
                    tt("dve", t2, pp[:, 512:1024], SSIN[:, hf * 512:(hf + 1) * 512], ALU.mult, r=[pk + "1", "ssinT"],
                       w=["G1:b"])
                    tt("dve", dst[:, hf * 512:(hf + 1) * 512], t1, t2, ALU.add, r=["G1:a", "G1:b"], w=[dkey])

        def load_cols(dst3, src2d, c0, n, key):
            dma_cast(dst3, src2d[:, c0:c0 + n].rearrange("(k p) n -> p k n", p=128), w=[key], append=True)

        def load_cols_swapped(dst3, src2d, bases, key):
            for bi, b0 in enumerate(bases):
                load_cols(dst3[:, :, bi * 64:bi * 64 + 32], src2d, b0 + 32, 32, key)
                load_cols(dst3[:, :, bi * 64 + 32:bi * 64 + 64], src2d, b0, 32, key)

        attn_q = []

        def attn_block(qT_ap, subs, vcols, o_ap, okey, rkeys, scale, sc_banks):
            groups = [subs[i:i + 4] for i in range(0, len(subs), 4)]
            for gi, grp in enumerate(groups):
                attn_q.append(dict(q=qT_ap, grp=grp, o=o_ap, okey=okey, rkeys=list(rkeys), scale=scale,
                                   first=(gi == 0), last=(gi == len(groups) - 1), after=None))

        def attn_after(fn):
            attn_q[-1]["after"] = fn

        def attn_flush():
            sc_banks = [(B4[:, 0:512], "B4"), (B5[:, 0:512], "B5"), (PA[:, 0:512], "PA0"), (PA[:, 512:1024], "PA1")]
            pt_bufs = [(9, "G9"), (10, "G10"), (0, "G0"), (1, "G1")]

            def emit_S(u):
                rot = attn_rot[0]
                attn_rot[0] += 1
                u["bank"], u["bkey"] = sc_banks[rot % 4]
                u["pt"] = gb(pt_bufs[rot % 4][0], BF16)[:, 0:512]
                u["pkey"] = pt_bufs[rot % 4][1]
                for si, (kT_ap, v_ap, mask, rk) in enumerate(u["grp"]):
                    mm(u["bank"][:, si * 128:(si + 1) * 128], kT_ap, u["q"], True, mask is None, r=u["rkeys"] + list(rk),
                       w=[u["bkey"]])
                    if mask is not None:
                        mt, mk = mask
                        mm(u["bank"][:, si * 128:(si + 1) * 128], ident[:], mt, False, True, r=["ident", mk], w=[u["bkey"]])

            def emit_rest(u):
                n = len(u["grp"])
                pt, pkey = u["pt"], u["pkey"]
                act(pt[:, 0:n * 128], u["bank"][:, 0:n * 128], AF.Exp, r=[u["bkey"]], w=[pkey], scale=u["scale"])
                for si, (kT_ap, v_ap, mask, rk) in enumerate(u["grp"]):
                    mm(u["o"], pt[:, si * 128:(si + 1) * 128], v_ap, u["first"] and si == 0, u["last"] and si == n - 1,
                       r=[pkey] + list(rk), w=[u["okey"]])
                if u["after"] is not None:
                    u["after"]()

            units = list(attn_q)
            del attn_q[:]
            if not units:
                return
            emit_S(units[0])
            for i, u in enumerate(units):
                if i + 1 < len(units):
                    emit_S(units[i + 1])
                emit_rest(u)

        attn_rot = [0]

        def load_w_out(l, w_out):
            if l == 0:
                dma_cast(WOUT[:, 0:4, :], w_out[0:512, :].rearrange("(k p) n -> p k n", p=128), w=["WOUT"])
                for c in range(4):
                    dma_cast(WOUT[0:64, 4 + c, :], w_out[512 + c * 64:512 + (c + 1) * 64, :], w=["WOUT"], append=True)
                    dma_cast(WOUT[64:128, 4 + c, :], w_out[512 + (4 + c) * 64:512 + (5 + c) * 64, :], w=["WOUT"],
                             append=True)
            else:
                dma_cast(WOUT[:], w_out.rearrange("(k p) n -> p k n", p=128), w=["WOUT"])

        def transpose_to_OT(src_fn, skey, chunk, ntiles=8):
            for t in range(ntiles):
                tr(PTB[:, t * 128:(t + 1) * 128], src_fn(t), r=[skey], w=["PTB"])
            cp("act", OT[:, chunk, :], PTB[:, 0:1024], r=["PTB"], w=["OT%d" % chunk])

        def make_ktm(kT_src, skey):
            km = [gb(2, BF16)[:, 0:1024], gb(2, BF16)[:, 1024:2048]]
            for hh_ in range(2):
                rws = slice(hh_ * 64, (hh_ + 1) * 64)
                cp("act", km[hh_][rws, :], kT_src[rws, :], r=[skey], w=["G2"])
            return km

        wg_i = [0]
        loaded = {}

        def do_load(item, w_in, latent):
            kind, idx = item
            W, wk = next_wg()
            if kind == "R":
                c = idx
                for wi in range(4):
                    load_cols(W[:, :, wi * 128:(wi + 1) * 128], w_in, wi * 512 + c * 128, 128, wk)
                if latent:
                    for wi in range(2):
                        load_cols_swapped(W[:, :, 512 + wi * 128: 640 + wi * 128], w_in,
                                          [wi * 512 + c * 128, wi * 512 + c * 128 + 64], wk)
            elif kind == "W":
                c = idx
                load_cols(W[:, :, 0:64], w_in, 2048 + c * 64, 64, wk)
                load_cols(W[:, :, 64:128], w_in, 2048 + (4 + c) * 64, 64, wk)
                if latent:
                    load_cols_swapped(W[:, :, 128:256], w_in, [2048 + c * 64, 2048 + (4 + c) * 64], wk)
                if c == 0:
                    load_cols(W[:, :, 256:512], w_in, 2560, 256, wk)
                    if latent:
                        load_cols_swapped(W[:, :, 512:640], w_in, [2560, 2624], wk)
            elif kind == "D":
                h = idx
                load_cols(W[:, :, 0:128], w_in, h * 128, 128, wk)
                load_cols(W[:, :, 256:384], w_in, 768 + h * 128, 128, wk)
                load_cols(W[:, :, 384:512], w_in, 1536 + h * 128, 128, wk)
                if latent:
                    for wi in range(2):
                        load_cols_swapped(W[:, :, 512 + wi * 128:640 + wi * 128], w_in,
                                          [wi * 768 + h * 128, wi * 768 + h * 128 + 64], wk)
            elif kind == "F":
                load_cols(W[:, :, 0:256], w_in, 2304, 256, wk)
            return W, wk

        def get_w(phase, l, order, pos, w_in, latent):
            key = (phase, l, pos)
            if key not in loaded:
                loaded[key] = do_load(order[pos], w_in, latent)
            res = loaded[key]
            nxt = (phase, l, pos + 1)
            if pos + 1 < len(order) and nxt not in loaded:
                loaded[nxt] = do_load(order[pos + 1], w_in, latent)
            return res

        def chk(name):
            P.marks.append((name, P.cnt["pe"]))
            if stop == name:
                raise StopBuild()

        def next_wg():
            i = wg_i[0] % 2
            wg_i[0] += 1
            return WG[i], "WG%d" % i

        for phase in range(2):
          try:
            latent = phase == 1
            tok0 = phase * 1024
            seqs = [(0, 1024)] if latent else [(s * 256, 256) for s in range(4)]
            for t in range(8):
                dma_sp(X[:, t, :], I["xin"][tok0 + t * 128: tok0 + (t + 1) * 128, :], w=["X%d" % t])

            for l in range(2):
                chk("start_p%d_l%d" % (phase, l))
                compute_mod(l, 0, phase)
                chk("mod1")
                ln_to_UT(0, phase, l)
                if phase == 0 and l == 0:
                    init_tables()
                chk("ut%d" % l)
                if stop == "utdump%d" % l:
                    for k in range(8):
                        cp("dve", X[:, k, :], UT[:, k, :], r=["UT%d" % t for t in range(8)], w=["X%d" % k])
                    raise StopBuild()
                w_in = I["w_in_ab"] if l == 0 else I["w_in_cd"]
                w_out = I["w_out_ab"] if l == 0 else I["w_out_cd"]
                if not (l == 1 and latent):
                    load_w_out(l, w_out)
                else:
                    dma_sp(WOUT[:], I["dftc1024"], w=["WOUT"])
                if l == 0:
                    order0 = [("R", c) for c in range(4)] + [("W", c) for c in range(4)]
                    for c in range(4):
                        W, wk = get_w(phase, l, order0, c, w_in, latent)
                        qT = gb(3, BF16)[:, 0:1024]; kT = gb(3, BF16)[:, 1024:2048]
                        proj_fm(qT, lambda k: W[:, k, 0:128], wk, 1024, "G3:qT", 0,
                                (lambda k: W[:, k, 512:640]) if latent else None)
                        proj_fm(kT, lambda k: W[:, k, 128:256], wk, 1024, "G3:kT", 1,
                                (lambda k: W[:, k, 640:768]) if latent else None)
                        chk("r_proj")
                        for d in range(2):
                            act(QD[:, d, :], FT[:, d, :], AF.Exp, r=["ftab", "LGC"], w=["QD"], scale=LGC[:, c, d:d + 1])
                        act(CDt[:], LGC[:, c, :], AF.Exp, r=["LGC"], w=["CDt"], scale=128.0)
                        qTf = gb(4, BF16)[:, 0:1024]; qTb = gb(4, BF16)[:, 1024:2048]
                        for d, dst in ((0, qTf), (1, qTb)):
                            tt("dve", dst.rearrange("p (n i) -> p n i", i=128), qT.rearrange("p (n i) -> p n i", i=128),
                               QD[:, d, :].unsqueeze(1).broadcast_to([128, 8, 128]), ALU.mult, r=["G3:qT", "QD"],
                               w=["G4:qTf%d" % d])
                        Vt = gb(5, BF16)[:, 0:1024].rearrange("p (t f) -> p t f", f=128)
                        Kt = gb(5, BF16)[:, 1024:2048].rearrange("p (t f) -> p t f", f=128)
                        SG = gb(6)[:].rearrange("p (t f) -> p t f", f=128)
                        for t in range(8):
                            tb, tbk = (B4, "B4") if t % 2 == 0 else (B5, "B5")
                            for k in range(8):
                                mm(tb[:, 0:256], UT[:, k, t * 128:(t + 1) * 128], W[:, k, 256:512], k == 0, k == 7,
                                   r=[wk, "UT%d" % t], w=[tbk])
                            cp("dve", Vt[:, t, :], tb[:, 0:128], r=[tbk], w=["G5:Vt"])
                            act(SG[:, t, :], tb[:, 128:256], AF.Silu, r=[tbk], w=["G6:SG"])
                            tr(PTB[:, t * 128:(t + 1) * 128], kT[:, t * 128:(t + 1) * 128], r=["G3:kT"], w=["PTB"])
                        cp("dve", Kt.rearrange("p t f -> p (t f)"), PTB[:, 0:1024], r=["PTB"], w=["G5:Kt"])
                        kTm = make_ktm(kT, "G3:kT")
                        chk("r_vg")
                        Kf = gb(7, BF16)[:, 0:1024].rearrange("p (t f) -> p t f", f=128)
                        Kb = gb(7, BF16)[:, 1024:2048].rearrange("p (t f) -> p t f", f=128)
                        for d, dst in ((0, Kf), (1, Kb)):
                            tt("dve", dst, Kt,
                               KDEC[:, d, 2 * c:2 * c + 2, :].rearrange("p h f -> p (h f)").unsqueeze(1).broadcast_to(
                                   [128, 8, 128]), ALU.mult, r=["G5:Kt", "KDEC"], w=["G7:Kf%d" % d])
                        chk("r_kf")
                        SST = gb(8, BF16)[:, 0:2048].rearrange("p (n d f) -> p n d f", d=2, f=128)
                        RO = gb(9)[:].rearrange("p (t f) -> p t f", f=128)
                        P.op("dve", lambda e: e.memset(gb(8, BF16)[:, 0:2048], 0.0), w=["G8:SST"])
                        nseq = len(seqs)
                        nch = seqs[0][1] // 128
                        skeys = ["S32_%d_%d" % (q, d) for q in range(nseq) for d in range(2)]
                        P.op("dve", lambda e, nseq=nseq: e.memset(S32A[:, 0:nseq], 0.0), w=skeys)
                        if latent:
                            for d in range(2):
                                for hh in range(2):
                                    dma_sp(S32A[hh * 64:(hh + 1) * 64, 0, d, hh * 64:(hh + 1) * 64],
                                           I["st_ret"][d, 2 * c + hh], r=[], w=["S32_0_%d" % d], append=True)
                        chain_banks = [(B5[:, 0:128], "B5"), (B6[:, 0:128], "B6"), (PA[:, 0:128], "PA0"),
                                       (PA[:, 512:640], "PA1"), (PB[:, 0:128], "PB0"), (PB[:, 512:640], "PB1"),
                                       (B4[:, 0:128], "B4"), (B5[:, 128:256], "B5")]
                        for step in range(nch):
                            for q, (s0, T) in enumerate(seqs):
                                t0 = s0 // 128
                                for d in range(2):
                                    n = step if d == 0 else nch - 1 - step
                                    skey = "S32_%d_%d" % (q, d)
                                    for hh in range(2):
                                        rws = slice(hh * 64, (hh + 1) * 64)
                                        cp("act", SST[rws, t0 + n, d, rws], S32A[rws, q, d, rws], r=[skey], w=["G8:SST"])
                                    if (not latent) or step < nch - 1:
                                        bank, bkey = chain_banks[q * 2 + d]
                                        Kd = Kf if d == 0 else Kb
                                        mm(bank, Kd[:, t0 + n, :], Vt[:, t0 + n, :], True, True,
                                           r=["G7:Kf%d" % d, "G5:Vt"], w=[bkey])
                                        stt(S32A[:, q, d, :], S32A[:, q, d, :], CDt[:, d:d + 1], bank, ALU.mult, ALU.add,
                                            r=[skey, "CDt", bkey], w=[skey])
                        chk("r_scan")
                        if not latent:
                            for q, (s0, T) in enumerate(seqs):
                                si = s0 // 256
                                for d in range(2):
                                    for hh in range(2):
                                        dma_sp(O["nstate"][si, d, 2 * c + hh],
                                               S32A[hh * 64:(hh + 1) * 64, q, d, hh * 64:(hh + 1) * 64],
                                               r=["S32_%d_%d" % (q, d)], w=["nstate"], append=True)
                        chk("r_state")
                        sc_b = [(B4[:, 256:512], "B4:s"), (B6[:, 256:512], "B6:s")]
                        o_b = [(B5[:, 256:384], "B5:o"), (PB[:, 256:384], "PB0:o")]
                        at_b = [(gb(10, BF16)[:, 0:256], "G10:a0"), (gb(10, BF16)[:, 256:512], "G10:a1")]
                        chunks = [s0 // 128 + n for (s0, T) in seqs for n in range(T // 128)]

                        def r_S(i):
                            t = chunks[i]
                            tsl = slice(t * 128, (t + 1) * 128)
                            bank, bkey = sc_b[i % 2]
                            for hh in range(2):
                                mm(bank[:, hh * 128:(hh + 1) * 128], kTm[hh][:, tsl], qT[:, tsl], True, True,
                                   r=["G2", "G3:qT"], w=[bkey])

                        def r_R(i):
                            t = chunks[i]
                            tsl = slice(t * 128, (t + 1) * 128)
                            bank, bkey = sc_b[i % 2]
                            AT, atk = at_b[i % 2]
                            ob, obk = o_b[i % 2]
                            tt("dve", AT, bank, DLALL[:, 2 * c:2 * c + 2, :].rearrange("p h i -> p (h i)"),
                               ALU.mult, r=[bkey, "DLALL"], w=[atk])
                            for hh in range(2):
                                rows = slice(hh * 64, (hh + 1) * 64)
                                osl = ob[:, hh * 64:(hh + 1) * 64]
                                mm(osl, AT[:, hh * 128:(hh + 1) * 128], Vt[:, t, rows], True, False, r=[atk, "G5:Vt"], w=[obk])
                                mm(osl, qTf[:, tsl], SST[:, t, 0, rows], False, False, r=["G4:qTf0", "G8:SST"], w=[obk])
                                mm(osl, qTb[:, tsl], SST[:, t, 1, rows], False, True, r=["G4:qTf1", "G8:SST"], w=[obk])
                            cp("dve", RO[:, t, :], ob, r=[obk], w=["G9:RO"])

                        r_S(0)
                        for i in range(len(chunks)):
                            if i + 1 < len(chunks):
                                r_S(i + 1)
                            r_R(i)
                        chk("r_out")
                        ROf = RO.rearrange("p t f -> p (t f)")
                        RO3 = ROf.rearrange("p (a f) -> p a f", f=64)
                        SQ = gb(10)[:]
                        act(SQ, ROf, AF.Square, r=["G9:RO"], w=["G10"])
                        red(sm[:, 16:32], RO3, r=["G9:RO"], w=["smA"])
                        red(sm[:, 32:48], SQ.rearrange("p (a f) -> p a f", f=64), r=["G10"], w=["smB"])
                        ts("dve", sm[:, 16:32], sm[:, 16:32], 1.0 / 64, None, ALU.mult, None, r=["smA"], w=["smA"])
                        tt("dve", sm[:, 48:64], sm[:, 16:32], sm[:, 16:32], ALU.mult, r=["smA"], w=["smC"])
                        stt(sm[:, 32:48], sm[:, 32:48], 1.0 / 64, sm[:, 48:64], ALU.mult, ALU.subtract, r=["smB", "smC"],
                            w=["smB"])
                        ts("dve", sm[:, 32:48], sm[:, 32:48], EPS, None, ALU.add, None, r=["smB"], w=["smB"])
                        rsqrt_pool(sm[:, 32:48], 16, "smB")
                        tt("dve", RO3, RO3, sm[:, 16:32].unsqueeze(2).broadcast_to([128, 16, 64]), ALU.subtract,
                           r=["G9:RO", "smA"], w=["G9:RO"])
                        tt("dve", RO3, RO3, sm[:, 32:48].unsqueeze(2).broadcast_to([128, 16, 64]), ALU.mult,
                           r=["G9:RO", "smB"], w=["G9:RO"])
                        tt("dve", RO, RO, GNG[:, c * 128:(c + 1) * 128].unsqueeze(1).broadcast_to([128, 8, 128]), ALU.mult,
                           r=["G9:RO", "GNG"], w=["G9:RO"])
                        tt("dve", RO, RO, GNB[:, c * 128:(c + 1) * 128].unsqueeze(1).broadcast_to([128, 8, 128]), ALU.add,
                           r=["G9:RO", "GNB"], w=["G9:RO"])
                        ROb = gb(10, BF16)[:, 0:1024].rearrange("p (t f) -> p t f", f=128)
                        tt("dve", ROb, RO, SG, ALU.mult, r=["G9:RO", "G6:SG"], w=["G10"])
                        chk("r_epi")
                        transpose_to_OT(lambda t: ROb[:, t, :], "G10", c)
                        chk("r_done")

                    chk("r_all")
                    kTw = gb(3, BF16)[:, 0:1024]
                    VE = gb(8, BF16)[:, 0:8 * 130].rearrange("p (t f) -> p t f", f=130)
                    CKT = [gb(4, BF16)[:, 0:512], gb(4, BF16)[:, 512:1024]]
                    CVE = gb(4, BF16)[:, 1024:1024 + 4 * 130].rearrange("p (t f) -> p t f", f=130)
                    for c in range(4):
                        W, wk = get_w(phase, l, order0, 4 + c, w_in, latent)
                        if c == 0:
                            proj_fm(kTw, lambda k: W[:, k, 256:384], wk, 1024, "G3:kTw", 0,
                                    (lambda k: W[:, k, 512:640]) if latent else None)
                            P.op("dve", lambda e, VE=VE: e.memset(VE, 1.0), w=["G8:VE"])
                            for t in range(8):
                                tb, tbk = (B4, "B4") if t % 2 == 0 else (B5, "B5")
                                vo = 0 if latent else 128
                                for k in range(8):
                                    if latent:
                                        mm(tb[:, 0:128], UT[:, k, t * 128:(t + 1) * 128], W[:, k, 384:512], k == 0, k == 7,
                                           r=[wk, "UT%d" % t], w=[tbk])
                                    else:
                                        mm(tb[:, 0:256], UT[:, k, t * 128:(t + 1) * 128], W[:, k, 256:512], k == 0, k == 7,
                                           r=[wk, "UT%d" % t], w=[tbk])
                                for kv in range(2):
                                    cp("dve", VE[:, t, kv * 65:kv * 65 + 64], tb[:, vo + kv * 64:vo + (kv + 1) * 64],
                                       r=[tbk], w=["G8:VE"])
                                if not latent:
                                    kvo = gb(5)[:, (t % 2) * 256:(t % 2) * 256 + 256]
                                    cp("act", kvo, tb[:, 0:256], r=[tbk], w=["G5:o%d" % (t % 2)])
                                    sq, tq = t // 2, (t % 2) * 128
                                    dma_sp(O["nwk"][sq, tq:tq + 128, :], kvo[:, 0:128], r=["G5:o%d" % (t % 2)], w=["nwk"],
                                           append=True)
                                    dma_sp(O["nwv"][sq, tq:tq + 128, :], kvo[:, 128:256], r=["G5:o%d" % (t % 2)], w=["nwv"],
                                           append=True)
                            kTm = make_ktm(kTw, "G3:kTw")
                            if latent:
                                ck = gb(5, BF16)[:, 0:512].rearrange("p (t f) -> p t f", f=128)
                                dma_cast(ck, I["cwk"].rearrange("(t p) f -> p t f", p=128), w=["G5:c"])
                                for t in range(4):
                                    tr(PTB[:, t * 128:(t + 1) * 128], ck[:, t, :], r=["G5:c"], w=["PTB"])
                                P.op("dve", lambda e: e.memset(gb(4, BF16)[:, 0:1024], 0.0), w=["G4:CKT"])
                                for hh_ in range(2):
                                    rws = slice(hh_ * 64, (hh_ + 1) * 64)
                                    cp("dve", CKT[hh_][rws, :], PTB[rws, 0:512], r=["PTB"], w=["G4:CKT"])
                                P.op("dve", lambda e, CVE=CVE: e.memset(CVE, 1.0), w=["G4:CVE"])
                                for kv in range(2):
                                    dma_cast(CVE[:, :, kv * 65:kv * 65 + 64],
                                             I["cwv"][:, kv * 64:(kv + 1) * 64].rearrange("(t p) f -> p t f", p=128),
                                             w=["G4:CVE"])
                        qT = gb(6, BF16)[:, 0:1024]
                        proj_fm(qT, lambda k: W[:, k, 0:128], wk, 1024, "G6:qTw", 1,
                                (lambda k: W[:, k, 128:256]) if latent else None)
                        WO = gb(7, BF16)[:, 0:1024].rearrange("p (t f) -> p t f", f=128)
                        for (s0, T) in seqs:
                            nch = T // 128
                            t0 = s0 // 128
                            for n in range(nch):
                                t = t0 + n
                                tsl = slice(t * 128, (t + 1) * 128)
                                for hh in range(2):
                                    rows = slice(hh * 64, (hh + 1) * 64)
                                    subs = []
                                    if latent:
                                        for cc in range(4):
                                            subs.append((CKT[hh][:, cc * 128:(cc + 1) * 128], CVE[:, cc, hh * 65:(hh + 1) * 65],
                                                         None, ["G4:CKT", "G4:CVE"]))
                                        for m in (n - 1, n, n + 1):
                                            if m < 0 or m >= nch:
                                                continue
                                            mask = None
                                            if m == n - 1:
                                                mask = (MPV[:], "mprev")
                                            elif m == n + 1:
                                                mask = (MNX[:], "mnext")
                                            subs.append((kTm[hh][:, (t0 + m) * 128:(t0 + m + 1) * 128],
                                                         VE[:, t0 + m, hh * 65:(hh + 1) * 65], mask, ["G2", "G8:VE"]))
                                    else:
                                        for m in range(nch):
                                            subs.append((kTm[hh][:, (t0 + m) * 128:(t0 + m + 1) * 128],
                                                         VE[:, t0 + m, hh * 65:(hh + 1) * 65], None, ["G2", "G8:VE"]))
                                    ob, obk = (B6, "B6") if t % 2 == 0 else (PB, "PB0")
                                    attn_block(qT[:, tsl], subs, 65, ob[:, hh * 65:(hh + 1) * 65], obk, ["G6:qTw"], 0.125, None)
                                def w_epi(t=t, ob=ob, obk=obk, c=c, WO=WO):
                                    o3 = ob[:, 0:130].rearrange("p (h f) -> p h f", f=65)
                                    tt("dve", sm[:, 0:2], o3[:, :, 64], ES[:, c:8:4], ALU.add, r=[obk, "ES"], w=["sm"])
                                    P.op("dve", lambda e: e.reciprocal(out=sm[:, 2:4], in_=sm[:, 0:2]), r=["sm"], w=["smr"])
                                    tt("dve", WO[:, t, :].rearrange("p (h f) -> p h f", f=64), o3[:, :, 0:64],
                                       sm[:, 2:4].unsqueeze(2).broadcast_to([128, 2, 64]), ALU.mult, r=[obk, "smr"],
                                       w=["G7:WO"])
                                attn_after(w_epi)
                        attn_flush()
                        transpose_to_OT(lambda t: WO[:, t, :], "G7:WO", 4 + c)
                        chk("w_done")
                else:
                    order1 = [("D", h) for h in range(6)] + [("F", 0)]
                    for h in range(6):
                        W, wk = get_w(phase, l, order1, h, w_in, latent)
                        qT = gb(3, BF16)[:, 0:1024]; kT = gb(3, BF16)[:, 1024:2048]
                        proj_fm(qT, lambda k: W[:, k, 0:128], wk, 1024, "G3:qT", 0,
                                (lambda k: W[:, k, 512:640]) if latent else None)
                        proj_fm(kT, lambda k: W[:, k, 256:384], wk, 1024, "G3:kT", 1,
                                (lambda k: W[:, k, 640:768]) if latent else None)
                        if stop == "d_w" and h == 0:
                            for k in range(8):
                                cp("dve", X[:, k, 0:768], W[:, k, :], r=[wk], w=["X%d" % k])
                            raise StopBuild()
                        if stop == "d_ut" and h == 0:
                            for k in range(8):
                                cp("dve", X[:, k, :], UT[:, k, :], r=["UT%d" % t for t in range(8)], w=["X%d" % k])
                            raise StopBuild()
                        VE = gb(5, BF16)[:, 0:8 * 130].rearrange("p (t f) -> p t f", f=130)
                        P.op("dve", lambda e, VE=VE: e.memset(VE, 1.0), w=["G5:VEd"])
                        for t in range(8):
                            tb, tbk = (B4, "B4") if t % 2 == 0 else (B5, "B5")
                            for k in range(8):
                                if latent:
                                    mm(tb[:, 0:128], UT[:, k, t * 128:(t + 1) * 128], W[:, k, 384:512], k == 0, k == 7,
                                       r=[wk, "UT%d" % t], w=[tbk])
                                else:
                                    mm(tb[:, 0:256], UT[:, k, t * 128:(t + 1) * 128], W[:, k, 256:512], k == 0, k == 7,
                                       r=[wk, "UT%d" % t], w=[tbk])
                            cp("dve", VE[:, t, 0:128], tb[:, 0:128] if latent else tb[:, 128:256], r=[tbk], w=["G5:VEd"])
                            if not latent:
                                kvo = gb(6)[:, (t % 2) * 256:(t % 2) * 256 + 256]
                                cp("act", kvo, tb[:, 0:256], r=[tbk], w=["G6:o%d" % (t % 2)])
                                sq, tq = t // 2, (t % 2) * 128
                                dma_sp(O["ndk"][sq, tq:tq + 128, h * 128:(h + 1) * 128], kvo[:, 0:128],
                                       r=["G6:o%d" % (t % 2)], w=["ndk"], append=True)
                                dma_sp(O["ndv"][sq, tq:tq + 128, h * 128:(h + 1) * 128], kvo[:, 128:256],
                                       r=["G6:o%d" % (t % 2)], w=["ndv"], append=True)
                        kTm = make_ktm(kT, "G3:kT")
                        chk("d_kv")
                        CKT = [gb(4, BF16)[:, 0:512], gb(4, BF16)[:, 512:1024]]
                        CVE = gb(4, BF16)[:, 1024:1024 + 4 * 130].rearrange("p (t f) -> p t f", f=130)
                        if latent:
                            ck = gb(6, BF16)[:, 0:512].rearrange("p (t f) -> p t f", f=128)
                            dma_cast(ck, I["cdk"][:, h * 128:(h + 1) * 128].rearrange("(t p) f -> p t f", p=128), w=["G6:c"])
                            for t in range(4):
                                tr(PTB[:, t * 128:(t + 1) * 128], ck[:, t, :], r=["G6:c"], w=["PTB"])
                            P.op("dve", lambda e: e.memset(gb(4, BF16)[:, 0:1024], 0.0), w=["G4:CKT"])
                            for hh_ in range(2):
                                rws = slice(hh_ * 64, (hh_ + 1) * 64)
                                cp("dve", CKT[hh_][rws, :], PTB[rws, 0:512], r=["PTB"], w=["G4:CKT"])
                            P.op("dve", lambda e, CVE=CVE: e.memset(CVE, 1.0), w=["G4:CVE"])
                            dma_cast(CVE[:, :, 0:128], I["cdv"][:, h * 128:(h + 1) * 128].rearrange("(t p) f -> p t f", p=128),
                                     w=["G4:CVE"])
                        AO = gb(7)[:].rearrange("p (t f) -> p t f", f=128)
                        for (s0, T) in seqs:
                            nch = T // 128
                            t0 = s0 // 128
                            for n in range(nch):
                                t = t0 + n
                                tsl = slice(t * 128, (t + 1) * 128)
                                for m in range(2):
                                    rows = slice(m * 64, (m + 1) * 64)
                                    subs = []
                                    for mm_ in range(nch):
                                        subs.append((kTm[m][:, (t0 + mm_) * 128:(t0 + mm_ + 1) * 128], VE[:, t0 + mm_, 0:129],
                                                     None, ["G2", "G5:VEd"]))
                                    if latent:
                                        for cc in range(4):
                                            subs.append((CKT[m][:, cc * 128:(cc + 1) * 128], CVE[:, cc, 0:129], None,
                                                         ["G4:CKT", "G4:CVE"]))
                                    ob, obk = (B6, "B6") if t % 2 == 0 else (PB, "PB0")
                                    attn_block(qT[:, tsl], subs, 129, ob[:, m * 129:(m + 1) * 129], obk, ["G3:qT"], 0.125, None)
                                def d_epi(t=t, ob=ob, obk=obk, AO=AO):
                                    o3 = ob[:, 0:258].rearrange("p (h f) -> p h f", f=129)
                                    P.op("dve", lambda e, o3=o3: e.reciprocal(out=sm[:, 0:2], in_=o3[:, :, 128]), r=[obk],
                                         w=["sm"])
                                    tt("dve", sm[:, 1:2], sm[:, 1:2], LAM[:, 2:3], ALU.mult, r=["sm", "LAM"], w=["sm"])
                                    a1 = gb(8)[:, 0:128]
                                    ts("dve", a1, o3[:, 0, 0:128], sm[:, 0:1], None, ALU.mult, None, r=[obk, "sm"], w=["G8"])
                                    stt(AO[:, t, :], o3[:, 1, 0:128], sm[:, 1:2], a1, ALU.mult, ALU.add,
                                        r=[obk, "sm", "G8"], w=["G7:AO"])
                                attn_after(d_epi)
                        attn_flush()
                        AOf = AO.rearrange("p t f -> p (t f)")
                        SQ = gb(8)[:]
                        act(SQ, AOf, AF.Square, r=["G7:AO"], w=["G8"])
                        red(sm[:, 16:24], SQ.rearrange("p (t f) -> p t f", f=128), r=["G8"], w=["smB"])
                        ts("dve", sm[:, 16:24], sm[:, 16:24], 1.0 / 128, EPS, ALU.mult, ALU.add, r=["smB"], w=["smB"])
                        rsqrt_pool(sm[:, 16:24], 8, "smB")
                        tt("dve", AO, AO, sm[:, 16:24].unsqueeze(2).broadcast_to([128, 8, 128]), ALU.mult, r=["G7:AO", "smB"],
                           w=["G7:AO"])
                        AOb = gb(8, BF16)[:, 0:1024].rearrange("p (t f) -> p t f", f=128)
                        tt("dve", AOb, AO, SUB[:].unsqueeze(1).broadcast_to([128, 8, 128]), ALU.mult, r=["G7:AO", "SUB"],
                           w=["G8"])
                        transpose_to_OT(lambda t: AOb[:, t, :], "G8", h)
                        chk("d_done")
                    W, wk = get_w(phase, l, order1, 6, w_in, latent)
                    ZT = gb(3, BF16)[:].rearrange("p (c t) -> p c t", c=2)
                    for cc in range(2):
                        proj_fm(ZT[:, cc, :], lambda k, cc=cc: W[:, k, cc * 128:(cc + 1) * 128], wk, 1024, "G3:ZT%d" % cc, cc, None)
                    Asb = gb(4, BF16)[:].rearrange("p (t f) -> p t f", f=256)
                    Bsb = gb(5, BF16)[:].rearrange("p (t f) -> p t f", f=256)
                    for t in range(8):
                        for cc in range(2):
                            mm(B4[:, cc * 128:(cc + 1) * 128], ZT[:, cc, t * 128:(t + 1) * 128], BDC[:], True, True,
                               r=["G3:ZT%d" % cc, "bdc"], w=["B4"])
                            mm(B4[:, 256 + cc * 128:256 + (cc + 1) * 128], ZT[:, cc, t * 128:(t + 1) * 128], BDS[:], True, True,
                               r=["G3:ZT%d" % cc, "bds"], w=["B4"])
                        cp("dve", Asb[:, t, :], B4[:, 0:256], r=["B4"], w=["G4:Asb"])
                        cp("act", Bsb[:, t, :], B4[:, 256:512], r=["B4"], w=["G5:Bsb"])
                    Yb = gb(6, BF16)[:].rearrange("p (t f) -> p t f", f=256)
                    if latent:
                        YA = gb(7)[:, :].rearrange("p (t f) -> p t f", f=256)
                        YA2 = gb(8)[:, :].rearrange("p (t f) -> p t f", f=256)
                        ST = [(gb(0, BF16), "G0"), (gb(1, BF16), "G1"), (gb(9, BF16), "G9"), (gb(10, BF16), "G10")]
                        for j, (stb, stk) in enumerate(ST):
                            dma_sp(stb[:, 0:2048].rearrange("p (t n) -> p t n", t=2), I["dfts1024"][:, 2 * j:2 * j + 2, :],
                                   w=[stk])
                        for tp in range(8):
                            for t in range(8):
                                mm(B5[:, 0:256], WOUT[:, t, tp * 128:(tp + 1) * 128], Asb[:, t, :], t == 0, t == 7,
                                   r=["WOUT", "G4:Asb"], w=["B5"])
                            ya = YA[:, tp, :] if tp < 4 else YA2[:, tp - 4, :]
                            cp("act", ya, B5[:, 0:256], r=["B5"], w=["G7" if tp < 4 else "G8"])
                        load_w_out(l, w_out)
                        for tp in range(8):
                            for t in range(8):
                                stb, stk = ST[t // 2]
                                mm(B5[:, 0:256], stb[:, (t % 2) * 1024 + tp * 128:(t % 2) * 1024 + (tp + 1) * 128], Bsb[:, t, :],
                                   t == 0, t == 7, r=[stk, "G5:Bsb"], w=["B5"])
                            ya = YA[:, tp, :] if tp < 4 else YA2[:, tp - 4, :]
                            tt("dve", Yb[:, tp, :], ya, B5[:, 0:256], ALU.add, r=["B5", "G7" if tp < 4 else "G8"], w=["G6:Yb"])
                    else:
                        for (s0, T) in seqs:
                            t0 = s0 // 128
                            for tp in range(2):
                                for t in range(2):
                                    mm(B5[:, 0:256], DF256[:, 0, t, tp * 128:(tp + 1) * 128], Asb[:, t0 + t, :], t == 0, False,
                                       r=["DF256", "G4:Asb"], w=["B5"])
                                for t in range(2):
                                    mm(B5[:, 0:256], DF256[:, 1, t, tp * 128:(tp + 1) * 128], Bsb[:, t0 + t, :], False, t == 1,
                                       r=["DF256", "G5:Bsb"], w=["B5"])
                                cp("dve", Yb[:, t0 + tp, :], B5[:, 0:256], r=["B5"], w=["G6:Yb"])
                    for cc in range(2):
                        transpose_to_OT(lambda t, cc=cc: Yb[:, t, cc * 128:(cc + 1) * 128], "G6:Yb", 6 + cc)

                chk("ot%d" % l)
                for t in range(8):
                    for hf in range(2):
                        for k in range(8):
                            mm(PA[:, hf * 512:(hf + 1) * 512], OT[:, k, t * 128:(t + 1) * 128], WOUT[:, k, hf * 512:(hf + 1) * 512],
                               k == 0, k == 7, r=["OT%d" % k, "WOUT"], w=["PA%d" % hf])
                    tmp = gb(0)
                    for hf in range(2):
                        tt("dve", tmp[:, hf * 512:(hf + 1) * 512], PA[:, hf * 512:(hf + 1) * 512], BC[0][:, hf * 512:(hf + 1) * 512],
                           ALU.mult, r=["PA%d" % hf, "BC0"], w=["G0"])
                    stt(X[:, t, :], X[:, t, :], ALPHA, tmp, ALU.mult, ALU.add, r=["X%d" % t, "G0"], w=["X%d" % t])
                if phase == 0:
                    mod_load(l, 1, 0, 0)
                    mod_load(l, 1, 1, 1)
                post_ln_all()
                chk("mix%d" % l)
                ffn_state = {}
                ffn_loaded = {}

                def ffn_load(s_, l=l):
                    if s_ in ffn_loaded:
                        return
                    W, wk = next_wg()
                    load_cols(W[:, :, 0:256], I["w_gate"][l], s_ * 256, 256, wk)
                    load_cols(W[:, :, 256:512], I["w_up"][l], s_ * 256, 256, wk)
                    wd = WDr[s_ % 2]; wdk = "WD%d" % (s_ % 2)
                    wds = WDS[s_ % 2]; wdsk = "WDS%d" % (s_ % 2)
                    dma_cast(wd[:], I["w_down"][l][s_ * 256:(s_ + 1) * 256, :].rearrange("(j p) n -> p j n", p=128),
                             w=[wdk])
                    ffn_loaded[s_] = (W, wk, wd, wdk, wds, wdsk)

                if phase == 1:
                    ffn_load(0)
                    ffn_load(1)
                compute_mod(l, 1, phase)
                if phase == 0:
                    ffn_load(0)
                    ffn_load(1)
                chk("mod2")
                ln_to_UT(1, phase, l)
                chk("ut2")
                def ffn_GU(s_, hf):
                    if hf == 0:
                        ffn_load(s_)
                        W, wk, wd, wdk, wds, wdsk = ffn_loaded[s_]
                        tt("dve", wds[:], wd[:], BC[0][:].unsqueeze(1).broadcast_to([128, 2, 1024]), ALU.mult,
                           r=[wdk, "BC0"], w=[wdsk])
                        ffn_state[s_] = (W, wk, wds, wdsk)
                    W, wk, wds, wdsk = ffn_state[s_]
                    ukeys = ["UT%d" % t for t in range(hf * 4, hf * 4 + 4)]
                    hT = gb(4 + hf, BF16)[:, 0:1024].rearrange("p (j t) -> p j t", j=2)
                    hk = "G%d" % (4 + hf)
                    for j in range(2):
                        pp = PA if j == 0 else PB
                        pk = "PA" if j == 0 else "PB"
                        for k in range(8):
                            mm(pp[:, 0:512], W[:, k, j * 128:(j + 1) * 128], UT[:, k, hf * 512:(hf + 1) * 512], k == 0, k == 7,
                               r=[wk] + ukeys, w=[pk + "0"])
                        for k in range(8):
                            mm(pp[:, 512:1024], W[:, k, 256 + j * 128:256 + (j + 1) * 128], UT[:, k, hf * 512:(hf + 1) * 512],
                               k == 0, k == 7, r=[wk] + ukeys, w=[pk + "1"])
                        sg = gb(3)[:, j * 512:(j + 1) * 512]
                        act(sg, pp[:, 0:512], AF.Silu, r=[pk + "0"], w=["G3:s%d" % j])
                        tt("dve", hT[:, j, :], sg, pp[:, 512:1024], ALU.mult, r=["G3:s%d" % j, pk + "1"], w=[hk])

                def ffn_D(s_, hf):
                    W, wk, wds, wdsk = ffn_state[s_]
                    hT = gb(4 + hf, BF16)[:, 0:1024].rearrange("p (j t) -> p j t", j=2)
                    hk = "G%d" % (4 + hf)
                    for tq in range(4):
                        t = hf * 4 + tq
                        for oh in range(2):
                            bank, bkey = ((B4[:, 0:512], "B4"), (B5[:, 0:512], "B5"),
                                          (B6[:, 0:512], "B6"))[(tq * 2 + oh) % 3]
                            for j in range(2):
                                mm(bank[:, 0:512], hT[:, j, tq * 128:(tq + 1) * 128], wds[:, j, oh * 512:(oh + 1) * 512],
                                   j == 0, j == 1, r=[hk, wdsk], w=[bkey])
                            if s_ == 0:
                                stt(X[:, t, oh * 512:(oh + 1) * 512], X[:, t, oh * 512:(oh + 1) * 512], ALPHA, bank[:, 0:512],
                                    ALU.mult, ALU.add, r=["X%d" % t, bkey], w=["X%d" % t])
                            else:
                                tt("dve", X[:, t, oh * 512:(oh + 1) * 512], X[:, t, oh * 512:(oh + 1) * 512], bank[:, 0:512],
                                   ALU.add, r=["X%d" % t, bkey], w=["X%d" % t])

                prev_u = None
                for s_ in range(11):
                    for hf in range(2):
                        ffn_GU(s_, hf)
                        if prev_u is not None:
                            ffn_D(*prev_u)
                        prev_u = (s_, hf)
                ffn_D(*prev_u)
                if phase == 0 and l == 0:
                    mod_load(1, 0, 0, 0)
                    mod_load(1, 0, 1, 1)
                post_ln_all()
                chk("l%d" % l)
            chk("p%d" % phase)
            for t in range(8):
                dma_sp(O["y"][tok0 + t * 128: tok0 + (t + 1) * 128, :], X[:, t, :], r=["X%d" % t], w=["y"])
          except StopBuild:
            for t in range(8):
                dma_sp(O["y"][tok0 + t * 128: tok0 + (t + 1) * 128, :], X[:, t, :], r=["X%d" % t], w=["y"])
            break
        P.emit()
    nc._marks = P.marks
    return nc


def make_in_maps(inp):
    consts = host_constants()
    maps = []
    f = lambda a: np.ascontiguousarray(np.asarray(a, dtype=np.float32))
    for i in range(NCORES):
        b = i // 4
        m = {}
        m["xin"] = np.concatenate([f(inp["x_prompt"][4 * i:4 * i + 4]).reshape(1024, 1024), f(inp["x_sample"][b])], 0)
        m["cond"] = np.stack([f(inp["c_ctx"]), f(inp["c"][b])], 0)
        m["st_ret"] = f(inp["state_ret"][b, 0])
        m["cwk"] = f(inp["cache_win_k"][b, 0]).reshape(512, 128)
        m["cwv"] = f(inp["cache_win_v"][b, 0]).reshape(512, 128)
        m["cdk"] = f(inp["cache_diff_k"][b, 0]).reshape(512, 768)
        m["cdv"] = f(inp["cache_diff_v"][b, 0]).reshape(512, 768)
        m["w_mod"] = f(inp["w_mod"]); m["b_mod"] = f(inp["b_mod"])
        m["ln_g"] = f(inp["ln_g"]); m["ln_b"] = f(inp["ln_b"])
        m["w_in_ab"] = f(inp["w_in_ab"][0]); m["w_out_ab"] = f(inp["w_out_ab"][0])
        m["lgam"] = f(inp["ret_log_gamma"][0]).reshape(16)
        m["gn_g"] = f(inp["ret_gn_g"][0]); m["gn_b"] = f(inp["ret_gn_b"][0])
        m["sink"] = f(inp["win_sink"][0])
        m["w_in_cd"] = f(inp["w_in_cd"][0]); m["w_out_cd"] = f(inp["w_out_cd"][0])
        m["dlam"] = f(inp["diff_lambda"][0]); m["subln"] = f(inp["diff_subln_g"][0])
        m["w_gate"] = f(inp["w_gate"]); m["w_up"] = f(inp["w_up"]); m["w_down"] = f(inp["w_down"])
        m.update(consts)
        maps.append(m)
    return maps


_NC_CACHE = {}


def kernel(**inp):
    if "nc" not in _NC_CACHE:
        _NC_CACHE["nc"] = build_nc()
    nc = _NC_CACHE["nc"]
    maps = make_in_maps(inp)
    res = run_bass_kernel_spmd(nc, maps, core_ids=list(range(NCORES)))
    R = res.results
    y_prompt = np.concatenate([R[i]["y"][0:1024].reshape(4, 256, 1024) for i in range(NCORES)], 0)
    y_sample = np.stack([R[0]["y"][1024:2048], R[4]["y"][1024:2048]], 0)
    nstate = np.concatenate([R[i]["nstate"] for i in range(NCORES)], 0).reshape(32, 1, 2, 8, 64, 64)
    nwk = np.concatenate([R[i]["nwk"] for i in range(NCORES)], 0).reshape(32, 1, 256, 2, 64)
    nwv = np.concatenate([R[i]["nwv"] for i in range(NCORES)], 0).reshape(32, 1, 256, 2, 64)
    ndk = np.concatenate([R[i]["ndk"] for i in range(NCORES)], 0).reshape(32, 1, 256, 6, 2, 64)
    ndv = np.concatenate([R[i]["ndv"] for i in range(NCORES)], 0).reshape(32, 1, 256, 6, 128)
    outs = (y_prompt, y_sample, nstate, nwk, nwv, ndk, ndv)
    return tuple(np.ascontiguousarray(o.astype(np.float32)) for o in outs)
```

```python
import math
from contextlib import ExitStack

import numpy as np
import ml_dtypes
import concourse.bass as bass
import concourse.mybir as mybir
from concourse.bass_utils import run_bass_kernel_spmd

F32 = mybir.dt.float32
BF16 = mybir.dt.bfloat16
ALU = mybir.AluOpType
AF = mybir.ActivationFunctionType
AX = mybir.AxisListType

NDMASEM = 24
D = 1024
DFF = 2816
ALPHA = 4.0 ** 0.25
EPS = 1e-5
NCORES = 8


class StopBuild(Exception):
    pass


class Prog:
    ENGS = ("pe", "act", "dve", "pool", "sp")

    def __init__(self, nc):
        self.nc = nc
        self.ops = {e: [] for e in self.ENGS}
        self.cnt = {e: 0 for e in self.ENGS}
        self.last_w = {}
        self.readers = {}
        self.known = {e: {} for e in self.ENGS}
        self.dma_cnt = [0] * NDMASEM
        self.dma_rr = 0
        self.bases = {}
        self.marks = []
        self.multi_w = {}

    def _deps(self, eng, reads, writes, append=False):
        need = {}

        def add(tok, same_ok):
            if tok is None:
                return
            sname, val, teng = tok
            if teng == eng and not same_ok:
                return
            if need.get(sname, 0) < val:
                need[sname] = val

        for k in reads:
            add(self.last_w.get(k), same_ok=(eng in ("act", "dve", "pool")))
            for t in self.multi_w.get(k, {}).values():
                add(t, same_ok=True)
        for k in writes:
            add(self.last_w.get(k), same_ok=False)
            if not append:
                for t in self.multi_w.get(k, {}).values():
                    add(t, same_ok=False)
            for t in self.readers.get(k, {}).values():
                add(t, same_ok=False)
            for k2 in self.bases.get(k.split(":")[0], ()):
                if k2 != k:
                    add(self.last_w.get(k2), same_ok=False)
                    for t in self.readers.get(k2, {}).values():
                        add(t, same_ok=False)
                    for t in self.multi_w.get(k2, {}).values():
                        add(t, same_ok=False)
        waits = []
        kn = self.known[eng]
        for sname, val in need.items():
            if kn.get(sname, 0) >= val:
                continue
            kn[sname] = val
            waits.append((sname, val))
        return waits

    def _commit(self, tok, reads, writes, append=False):
        for k in list(reads) + list(writes):
            self.bases.setdefault(k.split(":")[0], set()).add(k)
        def put(dct, k):
            d2 = dct.setdefault(k, {})
            key2 = (tok[0], tok[2])
            old = d2.get(key2)
            if old is None or old[1] < tok[1]:
                d2[key2] = tok
        for k in reads:
            put(self.readers, k)
        for k in writes:
            if append:
                put(self.multi_w, k)
            else:
                self.last_w[k] = tok
                self.readers[k] = {}
                self.multi_w[k] = {}

    PSUM_BASES = ("PA0", "PA1", "PB0", "PB1", "B4", "B5", "B6", "PTB")

    def op(self, eng, fn, r=(), w=(), signal=True):
        pr = [k for k in r if k.split(":")[0] in self.PSUM_BASES]
        if pr:
            r = [k for k in r if k not in pr]
            w = list(w) + pr
        waits = self._deps(eng, r, w)
        if signal:
            self.cnt[eng] += 1
            tok = ("c_" + eng, self.cnt[eng], eng)
            self.ops[eng].append((waits, fn, ("c_" + eng, 1)))
        else:
            tok = ("c_" + eng, self.cnt[eng] + 1, eng)
            self.ops[eng].append((waits, fn, None))
        self._commit(tok, r, w)

    def dma(self, q, fn, r=(), w=(), append=False):
        waits = self._deps(q, r, w, append)
        si = self.dma_rr % NDMASEM
        self.dma_rr += 1
        sname = "d%d" % si
        prev = self.dma_cnt[si]
        if prev > 0 and self.known[q].get(sname, 0) < prev:
            self.known[q][sname] = prev
            waits.append((sname, prev))
        self.dma_cnt[si] = prev + 16
        tok = (sname, prev + 16, "dma")
        self.ops[q].append((waits, fn, (sname, 16)))
        self._commit(tok, r, w, append)

    def emit(self):
        nc = self.nc
        with ExitStack() as st:
            sems = {}
            for e in ("pe", "act", "dve", "pool"):
                sems["c_" + e] = st.enter_context(nc.semaphore("c_" + e))
            for i in range(NDMASEM):
                sems["d%d" % i] = st.enter_context(nc.semaphore("d%d" % i))
            fin = []
            for e in ("pe", "act", "dve", "pool"):
                if self.cnt[e]:
                    fin.append(("c_" + e, self.cnt[e]))
            for i in range(NDMASEM):
                if self.dma_cnt[i]:
                    fin.append(("d%d" % i, self.dma_cnt[i]))
            block = st.enter_context(nc.Block())

            def run(name, eng):
                for waits, fn, inc in self.ops[name]:
                    for sname, val in waits:
                        eng.wait_ge(sems[sname], val)
                    ins = fn(eng)
                    if inc is not None:
                        ins.then_inc(sems[inc[0]], inc[1])

            @block.tensor
            def _(e):
                run("pe", e)

            @block.scalar
            def _(e):
                run("act", e)

            @block.vector
            def _(e):
                run("dve", e)

            @block.gpsimd
            def _(e):
                run("pool", e)

            @block.sync
            def _(e):
                run("sp", e)
                for sname, val in fin:
                    e.wait_ge(sems[sname], val)


def host_constants():
    c = {}
    t = np.arange(1024)
    row = (t // 64).astype(np.float64)
    col = (t % 64).astype(np.float64)
    inv = 10000.0 ** (-np.arange(16, dtype=np.float64) / 16)
    ang = np.concatenate([row[:, None] * inv, col[:, None] * inv], -1)
    cosT = np.zeros((128, 1024), np.float32)
    ssinT = np.zeros((128, 1024), np.float32)
    for r in range(128):
        d = r % 64
        j = d % 32
        cosT[r] = np.cos(ang[:, j])
        ssinT[r] = (-1.0 if d < 32 else 1.0) * np.sin(ang[:, j])
    c["cosT"] = cosT
    c["ssinT"] = ssinT
    i = np.arange(128, dtype=np.float64)
    ftab = np.zeros((128, 2, 128), np.float32)
    ftab[:, 0, :] = (i + 1)[None, :]
    ftab[:, 1, :] = (128 - i)[None, :]
    c["ftab"] = ftab
    etab = np.zeros((128, 2), np.float32)
    etab[:, 0] = 127 - i
    etab[:, 1] = i
    c["etab"] = etab
    jj = i[:, None]
    ii = i[None, :]
    c["rp"] = np.maximum(ii - jj, 0).astype(np.float32)
    c["rn"] = np.maximum(jj - ii, 0).astype(np.float32)
    c["ml"] = ((ii >= jj) * 0.125).astype(np.float32)
    c["mu"] = ((jj >= ii) * 0.125).astype(np.float32)
    c["mprev"] = np.where(jj >= ii, 0.0, -30000.0).astype(np.float32).astype(ml_dtypes.bfloat16)
    c["mnext"] = np.where(jj <= ii, 0.0, -30000.0).astype(np.float32).astype(ml_dtypes.bfloat16)
    f = np.arange(64, dtype=np.float64)
    a64 = 2 * np.pi * np.outer(f, f) / 64
    bdc = np.zeros((128, 128)); bds = np.zeros((128, 128))
    for g in range(2):
        bdc[g * 64:(g + 1) * 64, g * 64:(g + 1) * 64] = np.cos(a64)
        bds[g * 64:(g + 1) * 64, g * 64:(g + 1) * 64] = np.sin(a64)
    c["bdc"] = bdc.astype(np.float32).astype(ml_dtypes.bfloat16)
    c["bds"] = bds.astype(np.float32).astype(ml_dtypes.bfloat16)
    for T in (256, 1024):
        tt = np.arange(T, dtype=np.float64)
        a = 2 * np.pi * np.outer(tt, tt) / T
        nrm = 1.0 / math.sqrt(T * 64)
        C = (np.cos(a) * nrm).reshape(T // 128, 128, T).transpose(1, 0, 2)
        S = (-np.sin(a) * nrm).reshape(T // 128, 128, T).transpose(1, 0, 2)
        c["dftc%d" % T] = np.ascontiguousarray(C).astype(np.float32).astype(ml_dtypes.bfloat16)
        c["dfts%d" % T] = np.ascontiguousarray(S).astype(np.float32).astype(ml_dtypes.bfloat16)
    return c


CONST_SPECS = [
    ("cosT", [128, 1024], F32), ("ssinT", [128, 1024], F32), ("ftab", [128, 2, 128], F32),
    ("etab", [128, 2], F32), ("rp", [128, 128], F32), ("rn", [128, 128], F32), ("ml", [128, 128], F32),
    ("mu", [128, 128], F32), ("mprev", [128, 128], BF16), ("mnext", [128, 128], BF16),
    ("bdc", [128, 128], BF16), ("bds", [128, 128], BF16),
    ("dftc256", [128, 2, 256], BF16), ("dfts256", [128, 2, 256], BF16),
    ("dftc1024", [128, 8, 1024], BF16), ("dfts1024", [128, 8, 1024], BF16),
]

IN_SPECS = [
    ("xin", [2048, 1024]), ("cond", [2, 1024]), ("st_ret", [2, 8, 64, 64]), ("cwk", [512, 128]),
    ("cwv", [512, 128]), ("cdk", [512, 768]), ("cdv", [512, 768]),
    ("w_mod", [2, 1024, 6144]), ("b_mod", [2, 6144]), ("ln_g", [2, 2, 1024]), ("ln_b", [2, 2, 1024]),
    ("w_in_ab", [1024, 2816]), ("w_out_ab", [1024, 1024]), ("lgam", [16]), ("gn_g", [512]), ("gn_b", [512]),
    ("sink", [8]), ("w_in_cd", [1024, 2560]), ("w_out_cd", [1024, 1024]), ("dlam", [4, 64]), ("subln", [128]),
    ("w_gate", [2, 1024, 2816]), ("w_up", [2, 1024, 2816]), ("w_down", [2, 2816, 1024]),
]
OUT_SPECS = [
    ("y", [2048, 1024]), ("nstate", [4, 2, 8, 64, 64]), ("nwk", [4, 256, 128]), ("nwv", [4, 256, 128]),
    ("ndk", [4, 256, 768]), ("ndv", [4, 256, 768]),
]


def build_nc(stop="full"):
    nc = bass.Bass("TRN2", target_bir_lowering=False)
    I = {}
    for name, shape in IN_SPECS:
        I[name] = nc.dram_tensor(name, shape, F32, kind="ExternalInput").ap()
    for name, shape, dt in CONST_SPECS:
        I[name] = nc.dram_tensor(name, shape, dt, kind="ExternalInput").ap()
    O = {}
    for name, shape in OUT_SPECS:
        O[name] = nc.dram_tensor(name, shape, F32, kind="ExternalOutput").ap()
    P = Prog(nc)
    st = ExitStack()
    with st:
        def sb(name, shape, dt=F32):
            return st.enter_context(nc.sbuf_tensor(name, shape, dt))

        def ps(name, shape, dt=F32):
            return st.enter_context(nc.psum_tensor(name, shape, dt))

        X = sb("X", [128, 8, 1024])
        UT = sb("UT", [128, 8, 1024], BF16)
        OT = sb("OT", [128, 8, 1024], BF16)
        WOUT = sb("WOUT", [128, 8, 1024], BF16)
        WG = [sb("WG%d" % i, [128, 8, 768], BF16) for i in range(2)]
        WDr = [sb("WD%d" % i, [128, 2, 1024], BF16) for i in range(2)]
        WDS = [sb("WDS%d" % i, [128, 2, 1024], BF16) for i in range(2)]
        BC = [sb("BC%d" % i, [128, 1024]) for i in range(3)]
        COS = sb("COS", [128, 1024])
        SSIN = sb("SSIN", [128, 1024])
        ident = sb("ident", [128, 128], BF16)
        mhalf = sb("mhalf", [128, 16])
        FT = sb("FT", [128, 2, 128]); ET = sb("ET", [128, 2])
        MPV = sb("MPV", [128, 128], BF16); MNX = sb("MNX", [128, 128], BF16)
        BDC = sb("BDC", [128, 128], BF16); BDS = sb("BDS", [128, 128], BF16)
        DF256 = sb("DF256", [128, 2, 2, 256], BF16)
        LGB = sb("LGB", [128, 16]); LGC = sb("LGC", [128, 4, 2])
        DLALL = sb("DLALL", [128, 8, 128], BF16)
        KDEC = sb("KDEC", [128, 2, 8, 64], BF16)
        QD = sb("QD", [128, 2, 128]); CDt = sb("CDt", [128, 2])
        GNG = sb("GNG", [128, 512]); GNB = sb("GNB", [128, 512])
        ES = sb("ES", [128, 8])
        LAM = sb("LAM", [128, 4]); SUB = sb("SUB", [128, 128])
        CONDT = sb("CONDT", [128, 2, 8]); SCB = sb("SCB", [128, 8, 2], BF16); SBC = sb("SBC", [128, 2, 8, 128], BF16)
        MODT = sb("MODT", [128, 2, 2, 4, 8])
        GSCR = nc.dram_tensor("gscr", [2, 2, 128, 1024], F32).ap()
        BMT = sb("BMT", [128, 8])
        BMT0 = sb("BMT0", [128, 2, 8])
        NG = 11
        G = [sb("G%d" % i, [128, 1024]) for i in range(NG)]
        sm = sb("sm", [128, 64])
        sm2 = sm
        RPt = G[1][:, 0:128]; RNt = G[1][:, 128:256]; MLt = G[1][:, 256:384]; MUt = G[1][:, 384:512]
        DLM = G[1][:, 512:768].rearrange("p (a b) -> p a b", a=4)
        S32A = sb("S32A", [128, 4, 2, 128])
        LNS = sb("LNS", [128, 8, 16])
        PA = ps("PA", [128, 1024]); PB = ps("PB", [128, 1024])
        B4 = ps("B4", [128, 512]); B5 = ps("B5", [128, 512]); B6 = ps("B6", [128, 512])
        PTB = ps("PTB", [128, 1024], BF16)

        def gb(i, dt=F32):
            a = G[i][:]
            return a if dt == F32 else a.bitcast(BF16)

        def dma_sp(out, in_, r=(), w=(), slow=False, append=False):
            P.dma("sp", lambda e: e.dma_start(out=out, in_=in_, allow_slow_non_contiguous=slow), r=r, w=w, append=append)

        def dma_cast(out, in_, r=(), w=(), append=False):
            P.dma("pool", lambda e: e.dma_start(out=out, in_=in_), r=r, w=w, append=append)

        def mm(out, lhsT, rhs, start, stop, r, w):
            P.op("pe", lambda e: e.matmul(out, lhsT=lhsT, rhs=rhs, start=start, stop=stop), r=r, w=w, signal=bool(stop))

        def tr(out, in_, r, w):
            P.op("pe", lambda e: e.transpose(out=out, in_=in_, identity=ident[:]), r=list(r) + ["ident"], w=w)

        def act(out, in_, func, r, w, scale=1.0, bias=0.0, accum=None):
            if accum is None:
                P.op("act", lambda e: e.activation(out=out, in_=in_, func=func, scale=scale, bias=bias), r=r, w=w)
            else:
                P.op("act", lambda e: e.activation(out=out, in_=in_, func=func, scale=scale, bias=bias,
                                                   accum_out=accum), r=r, w=w)

        def tt(eng, out, in0, in1, op, r, w):
            P.op(eng, lambda e: e.tensor_tensor(out=out, in0=in0, in1=in1, op=op), r=r, w=w)

        def ts(eng, out, in0, s1, s2, op0, op1, r, w):
            if s2 is None:
                P.op(eng, lambda e: e.tensor_scalar(out=out, in0=in0, scalar1=s1, scalar2=None, op0=op0), r=r, w=w)
            else:
                P.op(eng, lambda e: e.tensor_scalar(out=out, in0=in0, scalar1=s1, scalar2=s2, op0=op0, op1=op1),
                     r=r, w=w)

        def stt(out, in0, scalar, in1, op0, op1, r, w):
            P.op("dve", lambda e: e.scalar_tensor_tensor(out=out, in0=in0, scalar=scalar, in1=in1, op0=op0, op1=op1),
                 r=r, w=w)

        def cp(eng, out, in_, r, w):
            if eng == "act":
                P.op("act", lambda e: e.copy(out=out, in_=in_), r=r, w=w)
            else:
                P.op(eng, lambda e: e.tensor_copy(out=out, in_=in_), r=r, w=w)

        def red(out, in_, r, w):
            P.op("dve", lambda e: e.tensor_reduce(out=out, in_=in_, axis=AX.X, op=ALU.add), r=r, w=w)

        def rsqrt_pool(ap, n, key):
            tt("pool", ap, ap, mhalf[:, 0:n], ALU.pow, r=[key, "mhalf"], w=[key])

        OTKEYS = ["OT%d" % k for k in range(8)]
        mod_bufs = [(WOUT, ["WOUT"]), (OT, OTKEYS)]
        mod_loaded = {}

        def mod_load(l, half, vi, bi):
            if (l, half, vi) in mod_loaded:
                return
            buf, keys = mod_bufs[bi]
            idx = 3 * half + vi
            dma_cast(buf[:], I["w_mod"][l][:, idx * 1024:(idx + 1) * 1024].rearrange("(k p) n -> p k n", p=128), w=keys)
            mod_loaded[(l, half, vi)] = (buf, keys)

        for c in range(2):
            dma_sp(CONDT[:, c, :], I["cond"][c].rearrange("(k p) -> p k", p=128), w=["CONDT"], slow=True)
        mod_load(0, 0, 0, 0)
        mod_load(0, 0, 1, 1)
        for vi_ in range(2):
            dma_sp(BMT0[:, vi_, :], I["b_mod"][0, vi_ * 1024:(vi_ + 1) * 1024].rearrange("(k p) -> p k", p=128),
                   w=["BMT0"], slow=True)
        act(SCB[:].rearrange("p k c -> p c k"), CONDT[:], AF.Silu, r=["CONDT"], w=["SCB"])
        P.op("pool", lambda e: e.memset(gb(2, BF16)[:, 0:2048], 0.0), w=["G2"])
        P.op("pool", lambda e: e.memset(ident[:], 0.0), w=["ident"])
        P.op("pool", lambda e: e.affine_select(out=ident[:], in_=ident[:], compare_op=ALU.not_equal, fill=1.0,
                                              base=0, pattern=[[-1, 128]], channel_multiplier=1),
             r=["ident"], w=["ident"])
        P.op("pool", lambda e: e.memset(mhalf[:], -0.5), w=["mhalf"])
        def init_tables():
            dma_sp(LGB[:], I["lgam"].partition_broadcast(128), w=["LGB"])
            for ap_, name in ((RPt, "rp"), (RNt, "rn"), (MLt, "ml"), (MUt, "mu")):
                dma_sp(ap_, I[name], w=["G1:" + name])
            for tile_, name in ((FT, "ftab"), (ET, "etab")):
                dma_sp(tile_[:], I[name], w=[name])
            for hh in range(2):
                for d in range(2):
                    dma_sp(LGC[hh * 64:(hh + 1) * 64, :, d],
                           I["lgam"][d * 8 + hh:d * 8 + 8:2].partition_broadcast(64), w=["LGC"], slow=True)
            dma_sp(GNG[:], I["gn_g"].partition_broadcast(128), w=["GNG"])
            dma_sp(GNB[:], I["gn_b"].partition_broadcast(128), w=["GNB"])
            for tile_, name in ((COS, "cosT"), (SSIN, "ssinT"), (MPV, "mprev"), (MNX, "mnext"), (BDC, "bdc"), (BDS, "bds")):
                dma_sp(tile_[:], I[name], w=[name])
            dma_sp(DF256[:, 0], I["dftc256"], w=["DF256"])
            dma_sp(DF256[:, 1], I["dfts256"], w=["DF256"])
            dma_sp(ES[:], I["sink"].partition_broadcast(128), w=["ES"])
            dma_sp(DLM[:].rearrange("p a b -> p (a b)"), I["dlam"].rearrange("a b -> (a b)").partition_broadcast(128),
                   w=["G1:DLM"])
            dma_sp(SUB[:], I["subln"].partition_broadcast(128), w=["SUB"])
            for h in range(8):
                e1 = gb(0)[:, 0:128]; e2 = gb(0)[:, 128:256]
                act(e1, RPt[:], AF.Exp, r=["G1:rp", "LGB"], w=["G0"], scale=LGB[:, h:h + 1])
                act(e2, RNt[:], AF.Exp, r=["G1:rn", "LGB"], w=["G0"], scale=LGB[:, 8 + h:9 + h])
                tt("dve", e1, e1, MLt[:], ALU.mult, r=["G0", "G1:ml"], w=["G0"])
                tt("dve", e2, e2, MUt[:], ALU.mult, r=["G0", "G1:mu"], w=["G0"])
                tt("dve", DLALL[:, h, :], e1, e2, ALU.add, r=["G0"], w=["DLALL"])
            for d in range(2):
                ts("dve", sm[:, d * 8:(d + 1) * 8], LGB[:, d * 8:(d + 1) * 8], ET[:, d:d + 1], None, ALU.mult, None,
                   r=["LGB", "etab"], w=["sm"])
            act(sm[:, 0:16], sm[:, 0:16], AF.Exp, r=["sm"], w=["sm"])
            ts("dve", KDEC[:].rearrange("p d h f -> p (d h) f"), sm[:, 0:16].unsqueeze(2).broadcast_to([128, 16, 64]),
               0.125, None, ALU.mult, None, r=["sm"], w=["KDEC"])
            act(ES[:], ES[:], AF.Exp, r=["ES"], w=["ES"])
            tt("dve", gb(0)[:, 0:64], DLM[:, 0, :], DLM[:, 1, :], ALU.mult, r=["G1:DLM"], w=["G0"])
            tt("dve", gb(0)[:, 64:128], DLM[:, 2, :], DLM[:, 3, :], ALU.mult, r=["G1:DLM"], w=["G0"])
            red(LAM[:, 0:2], gb(0)[:, 0:128].rearrange("p (a b) -> p a b", a=2), r=["G0"], w=["LAM"])
            act(LAM[:, 0:2], LAM[:, 0:2], AF.Exp, r=["LAM"], w=["LAM"])
            lam_init = 0.8 - 0.6 * math.exp(-0.3 * 1)
            tt("dve", LAM[:, 2:3], LAM[:, 1:2], LAM[:, 0:1], ALU.subtract, r=["LAM"], w=["LAM"])
            ts("dve", LAM[:, 2:3], LAM[:, 2:3], -lam_init, None, ALU.add, None, r=["LAM"], w=["LAM"])
            ts("dve", SUB[:], SUB[:], 1.0 - lam_init, None, ALU.mult, None, r=["SUB"], w=["SUB"])


        def ln_stats(src, rkeys):
            for hh in range(2):
                P.op("dve", lambda e, hh=hh: e.bn_stats(out=sm2[:, hh * 6:(hh + 1) * 6],
                                                        in_=src[:, hh * 512:(hh + 1) * 512]),
                     r=rkeys, w=["sm2a"])
            P.op("dve", lambda e: e.bn_aggr(out=sm2[:, 16:18], in_=sm2[:, 0:12]), r=["sm2a"], w=["sm2b"])
            ts("dve", sm2[:, 18:19], sm2[:, 17:18], EPS, None, ALU.add, None, r=["sm2b"], w=["sm2c"])
            rsqrt_pool(sm2[:, 18:19], 1, "sm2c")
            return sm2[:, 16:17], sm2[:, 18:19]

        def compute_mod(l, half, phase):
            if phase == 1:
                dma_sp(BC[0][:], GSCR[l, half], r=["GSCR%d%d" % (l, half)], w=["BC0"])
                dma_sp(BC[1][:], I["ln_g"][l, half].partition_broadcast(128), w=["BC1"])
                dma_sp(BC[2][:], I["ln_b"][l, half].partition_broadcast(128), w=["BC2"])
                return
            if l == 0 and half == 0:
                for c in range(2):
                    cp("dve", SBC[:, c], SCB[:, :, c].unsqueeze(2).broadcast_to([128, 8, 128]), r=["SCB"], w=["SBC"])
            mod_load(l, half, 0, 0)
            mod_load(l, half, 1, 1)
            for vi in range(3):
                idx = 3 * half + vi
                if vi == 2:
                    mod_load(l, half, 2, 0)
                WB, wbk = mod_loaded[(l, half, vi)]
                if vi < 2:
                    if l == 0 and half == 0:
                        bmt, bmk = BMT0[:, vi, :], "BMT0"
                    else:
                        bmt, bmk = BMT[:], "BMT"
                        dma_sp(BMT[:], I["b_mod"][l, idx * 1024:(idx + 1) * 1024].rearrange("(k p) -> p k", p=128),
                               w=["BMT"], slow=True)
                    for j in range(8):
                        for k in range(8):
                            mm(B4[:, 2 * j:2 * j + 2], WB[:, k, j * 128:(j + 1) * 128], SCB[:, k, :], k == 0, k == 7,
                               r=wbk + ["SCB"], w=["B4"])
                    for c in range(2):
                        if vi == 0:
                            tt("dve", MODT[:, c, l, 2 * half + vi, :], B4[:, c:16:2], bmt, ALU.add, r=["B4", bmk],
                               w=["MODT"])
                        else:
                            stt(MODT[:, c, l, 2 * half + vi, :], B4[:, c:16:2], 1.0, bmt, ALU.add, ALU.add,
                                r=["B4", bmk], w=["MODT"])
                else:
                    dma_sp(BC[0][:], I["b_mod"][l, idx * 1024:(idx + 1) * 1024].partition_broadcast(128), w=["BC0"])
                    g1 = gb(0)
                    for c in range(2):
                        pp = PA if c == 0 else PB
                        pk = "PA" if c == 0 else "PB"
                        for hf in range(2):
                            for k in range(8):
                                mm(pp[:, hf * 512:(hf + 1) * 512], SBC[:, c, k, :], WB[:, k, hf * 512:(hf + 1) * 512],
                                   k == 0, k == 7, r=["SBC"] + wbk, w=["%s%d" % (pk, hf)])
                    for hf in range(2):
                        tt("dve", g1[:, hf * 512:(hf + 1) * 512], BC[0][:, hf * 512:(hf + 1) * 512],
                           PB[:, hf * 512:(hf + 1) * 512], ALU.add, r=["BC0", "PB%d" % hf], w=["G0"])
                    dma_sp(GSCR[l, half], g1, r=["G0"], w=["GSCR%d%d" % (l, half)])
                    for hf in range(2):
                        tt("dve", BC[0][:, hf * 512:(hf + 1) * 512], BC[0][:, hf * 512:(hf + 1) * 512],
                           PA[:, hf * 512:(hf + 1) * 512], ALU.add, r=["BC0", "PA%d" % hf], w=["BC0"])
            dma_sp(BC[1][:], I["ln_g"][l, half].partition_broadcast(128), w=["BC1"])
            dma_sp(BC[2][:], I["ln_b"][l, half].partition_broadcast(128), w=["BC2"])

        def ln_stats_all():
            junk = gb(9, BF16)[:, 0:1024]
            for t in range(4):
                act(junk, X[:, t, :], AF.Identity, r=["X%d" % t], w=["G9", "sm"], accum=sm[:, t:t + 1])
                act(junk, X[:, t, :], AF.Square, r=["X%d" % t], w=["G9", "sm"], accum=sm[:, 4 + t:5 + t])
            for t in range(4, 8):
                for hh in range(2):
                    P.op("dve", lambda e, t=t, hh=hh: e.bn_stats(out=LNS[:, t, hh * 6:(hh + 1) * 6],
                                                                 in_=X[:, t, hh * 512:(hh + 1) * 512]),
                         r=["X%d" % t], w=["LNS%d" % t])
                P.op("dve", lambda e, t=t: e.bn_aggr(out=LNS[:, t, 12:14], in_=LNS[:, t, 0:12]), r=["LNS%d" % t],
                     w=["LNS%d" % t])
            ts("dve", LNS[:, 0:4, 12], sm[:, 0:4], 1.0 / 1024, None, ALU.mult, None, r=["sm"], w=["LNS%d" % t for t in range(4)])
            tt("dve", sm[:, 8:12], LNS[:, 0:4, 12], LNS[:, 0:4, 12], ALU.mult, r=["LNS%d" % t for t in range(4)], w=["sm"])
            stt(LNS[:, 0:4, 13], sm[:, 4:8], 1.0 / 1024, sm[:, 8:12], ALU.mult, ALU.subtract, r=["sm"],
                w=["LNS%d" % t for t in range(4)])
            ts("dve", LNS[:, :, 14], LNS[:, :, 13], EPS, None, ALU.add, None, r=["LNS%d" % t for t in range(8)], w=["LNSr"])
            tt("pool", LNS[:, :, 14], LNS[:, :, 14], mhalf[:, 0:8], ALU.pow, r=["LNSr", "mhalf"], w=["LNSr"])

        def ln_to_UT(half, c, l):
            ln_stats_all()
            for t in range(8):
                uh = gb(t % 2, BF16)[:, 0:1024]
                uk = "G%d" % (t % 2)
                ts("dve", uh, X[:, t, :], LNS[:, t, 12:13], LNS[:, t, 14:15], ALU.subtract, ALU.mult,
                   r=["X%d" % t, "LNS%d" % t, "LNSr"], w=[uk])
                for k in range(8):
                    tr(PTB[:, k * 128:(k + 1) * 128], uh[:, k * 128:(k + 1) * 128], r=[uk], w=["PTB"])
                for k in range(4):
                    act(UT[:, k, t * 128:(t + 1) * 128], PTB[:, k * 128:(k + 1) * 128], AF.Identity,
                        r=["PTB", "MODT"], w=["UT%d" % t], scale=MODT[:, c, l, 2 * half + 1, k:k + 1],
                        bias=MODT[:, c, l, 2 * half, k:k + 1])
                for k in range(4, 8):
                    ts("dve", UT[:, k, t * 128:(t + 1) * 128], PTB[:, k * 128:(k + 1) * 128],
                       MODT[:, c, l, 2 * half + 1, k:k + 1], MODT[:, c, l, 2 * half, k:k + 1], ALU.mult, ALU.add,
                       r=["PTB", "MODT"], w=["UT%d" % t])

        def post_ln_all():
            ln_stats_all()
            for t in range(8):
                stt(X[:, t, :], X[:, t, :], LNS[:, t, 12:13], BC[1][:], ALU.subtract, ALU.mult,
                    r=["X%d" % t, "LNS%d" % t, "BC1"], w=["X%d" % t])
                stt(X[:, t, :], X[:, t, :], LNS[:, t, 14:15], BC[2][:], ALU.mult, ALU.add,
                    r=["X%d" % t, "LNSr", "BC2"], w=["X%d" % t])

        def proj_fm(dst, wap_fn, wkey, ntok, dkey, psel=0, rope=None):
            nh = ntok // 512
            for hf in range(nh):
                pp = PA if (hf + psel) % 2 == 0 else PB
                pk = "PA" if (hf + psel) % 2 == 0 else "PB"
                ukeys = ["UT%d" % t for t in range(hf * 4, hf * 4 + 4)]
                for k in range(8):
                    mm(pp[:, 0:512], wap_fn(k), UT[:, k, hf * 512:(hf + 1) * 512], k == 0, k == 7,
                       r=[wkey] + ukeys, w=[pk + "0"])
                if rope is None:
                    cp("act", dst[:, hf * 512:(hf + 1) * 512], pp[:, 0:512], r=[pk + "0"], w=[dkey])
                else:
                    for k in range(8):
                        mm(pp[:, 512:1024], rope(k), UT[:, k, hf * 512:(hf + 1) * 512], k == 0, k == 7,
                           r=[wkey] + ukeys, w=[pk + "1"])
                    t1 = gb(1)[:, 0:512]; t2 = gb(1)[:, 512:1024]
                    tt("dve", t1, pp[:, 0:512], COS[:, hf * 512:(hf + 1) * 512], ALU.mult, r=[pk + "0", "cosT"], w=["G1:a"])
                    tt("dve", t2, pp[:, 512:1024], SSIN[:, hf * 512:(hf + 1) * 512], ALU.mult, r=[pk + "1", "ssinT"],
                       w=["G1:b"])
                    tt("dve", dst[:, hf * 512:(hf + 1) * 512], t1, t2, ALU.add, r=["G1:a", "G1:b"], w=[dkey])

        def load_cols(dst3, src2d, c0, n, key):
            dma_cast(dst3, src2d[:, c0:c0 + n].rearrange("(k p) n -> p k n", p=128), w=[key], append=True)

        def load_cols_swapped(dst3, src2d, bases, key):
            for bi, b0 in enumerate(bases):
                load_cols(dst3[:, :, bi * 64:bi * 64 + 32], src2d, b0 + 32, 32, key)
                load_cols(dst3[:, :, bi * 64 + 32:bi * 64 + 64], src2d, b0, 32, key)

        attn_q = []

        def attn_block(qT_ap, subs, vcols, o_ap, okey, rkeys, scale, sc_banks):
            groups = [subs[i:i + 4] for i in range(0, len(subs), 4)]
            for gi, grp in enumerate(groups):
                attn_q.append(dict(q=qT_ap, grp=grp, o=o_ap, okey=okey, rkeys=list(rkeys), scale=scale,
                                   first=(gi == 0), last=(gi == len(groups) - 1), after=None))

        def attn_after(fn):
            attn_q[-1]["after"] = fn

        def attn_flush():
            sc_banks = [(B4[:, 0:512], "B4"), (B5[:, 0:512], "B5"), (PA[:, 0:512], "PA0"), (PA[:, 512:1024], "PA1")]
            pt_bufs = [(9, "G9"), (10, "G10"), (0, "G0"), (1, "G1")]

            def emit_S(u):
                rot = attn_rot[0]
                attn_rot[0] += 1
                u["bank"], u["bkey"] = sc_banks[rot % 4]
                u["pt"] = gb(pt_bufs[rot % 4][0], BF16)[:, 0:512]
                u["pkey"] = pt_bufs[rot % 4][1]
                for si, (kT_ap, v_ap, mask, rk) in enumerate(u["grp"]):
                    mm(u["bank"][:, si * 128:(si + 1) * 128], kT_ap, u["q"], True, mask is None, r=u["rkeys"] + list(rk),
                       w=[u["bkey"]])
                    if mask is not None:
                        mt, mk = mask
                        mm(u["bank"][:, si * 128:(si + 1) * 128], ident[:], mt, False, True, r=["ident", mk], w=[u["bkey"]])

            def emit_rest(u):
                n = len(u["grp"])
                pt, pkey = u["pt"], u["pkey"]
                act(pt[:, 0:n * 128], u["bank"][:, 0:n * 128], AF.Exp, r=[u["bkey"]], w=[pkey], scale=u["scale"])
                for si, (kT_ap, v_ap, mask, rk) in enumerate(u["grp"]):
                    mm(u["o"], pt[:, si * 128:(si + 1) * 128], v_ap, u["first"] and si == 0, u["last"] and si == n - 1,
                       r=[pkey] + list(rk), w=[u["okey"]])
                if u["after"] is not None:
                    u["after"]()

            units = list(attn_q)
            del attn_q[:]
            if not units:
                return
            emit_S(units[0])
            for i, u in enumerate(units):
                if i + 1 < len(units):
                    emit_S(units[i + 1])
                emit_rest(u)

        attn_rot = [0]

        def load_w_out(l, w_out):
            if l == 0:
                dma_cast(WOUT[:, 0:4, :], w_out[0:512, :].rearrange("(k p) n -> p k n", p=128), w=["WOUT"])
                for c in range(4):
                    dma_cast(WOUT[0:64, 4 + c, :], w_out[512 + c * 64:512 + (c + 1) * 64, :], w=["WOUT"], append=True)
                    dma_cast(WOUT[64:128, 4 + c, :], w_out[512 + (4 + c) * 64:512 + (5 + c) * 64, :], w=["WOUT"],
                             append=True)
            else:
                dma_cast(WOUT[:], w_out.rearrange("(k p) n -> p k n", p=128), w=["WOUT"])

        def transpose_to_OT(src_fn, skey, chunk, ntiles=8):
            for t in range(ntiles):
                tr(PTB[:, t * 128:(t + 1) * 128], src_fn(t), r=[skey], w=["PTB"])
            cp("act", OT[:, chunk, :], PTB[:, 0:1024], r=["PTB"], w=["OT%d" % chunk])

        def make_ktm(kT_src, skey):
            km = [gb(2, BF16)[:, 0:1024], gb(2, BF16)[:, 1024:2048]]
            for hh_ in range(2):
                rws = slice(hh_ * 64, (hh_ + 1) * 64)
                cp("act", km[hh_][rws, :], kT_src[rws, :], r=[skey], w=["G2"])
            return km

        wg_i = [0]
        loaded = {}

        def do_load(item, w_in, latent):
            kind, idx = item
            W, wk = next_wg()
            if kind == "R":
                c = idx
                for wi in range(4):
                    load_cols(W[:, :, wi * 128:(wi + 1) * 128], w_in, wi * 512 + c * 128, 128, wk)
                if latent:
                    for wi in range(2):
                        load_cols_swapped(W[:, :, 512 + wi * 128: 640 + wi * 128], w_in,
                                          [wi * 512 + c * 128, wi * 512 + c * 128 + 64], wk)
            elif kind == "W":
                c = idx
                load_cols(W[:, :, 0:64], w_in, 2048 + c * 64, 64, wk)
                load_cols(W[:, :, 64:128], w_in, 2048 + (4 + c) * 64, 64, wk)
                if latent:
                    load_cols_swapped(W[:, :, 128:256], w_in, [2048 + c * 64, 2048 + (4 + c) * 64], wk)
                if c == 0:
                    load_cols(W[:, :, 256:512], w_in, 2560, 256, wk)
                    if latent:
                        load_cols_swapped(W[:, :, 512:640], w_in, [2560, 2624], wk)
            elif kind == "D":
                h = idx
                load_cols(W[:, :, 0:128], w_in, h * 128, 128, wk)
                load_cols(W[:, :, 256:384], w_in, 768 + h * 128, 128, wk)
                load_cols(W[:, :, 384:512], w_in, 1536 + h * 128, 128, wk)
                if latent:
                    for wi in range(2):
                        load_cols_swapped(W[:, :, 512 + wi * 128:640 + wi * 128], w_in,
                                          [wi * 768 + h * 128, wi * 768 + h * 128 + 64], wk)
            elif kind == "F":
                load_cols(W[:, :, 0:256], w_in, 2304, 256, wk)
            return W, wk

        def get_w(phase, l, order, pos, w_in, latent):
            key = (phase, l, pos)
            if key not in loaded:
                loaded[key] = do_load(order[pos], w_in, latent)
            res = loaded[key]
            nxt = (phase, l, pos + 1)
            if pos + 1 < len(order) and nxt not in loaded:
                loaded[nxt] = do_load(order[pos + 1], w_in, latent)
            return res

        def chk(name):
            P.marks.append((name, P.cnt["pe"]))
            if stop == name:
                raise StopBuild()

        def next_wg():
            i = wg_i[0] % 2
            wg_i[0] += 1
            return WG[i], "WG%d" % i

        for phase in range(2):
          try:
            latent = phase == 1
            tok0 = phase * 1024
            seqs = [(0, 1024)] if latent else [(s * 256, 256) for s in range(4)]
            for t in range(8):
                dma_sp(X[:, t, :], I["xin"][tok0 + t * 128: tok0 + (t + 1) * 128, :], w=["X%d" % t])

            for l in range(2):
                chk("start_p%d_l%d" % (phase, l))
                compute_mod(l, 0, phase)
                chk("mod1")
                ln_to_UT(0, phase, l)
                if phase == 0 and l == 0:
                    init_tables()
                chk("ut%d" % l)
                if stop == "utdump%d" % l:
                    for k in range(8):
                        cp("dve", X[:, k, :], UT[:, k, :], r=["UT%d" % t for t in range(8)], w=["X%d" % k])
                    raise StopBuild()
                w_in = I["w_in_ab"] if l == 0 else I["w_in_cd"]
                w_out = I["w_out_ab"] if l == 0 else I["w_out_cd"]
                if not (l == 1 and latent):
                    load_w_out(l, w_out)
                else:
                    dma_sp(WOUT[:], I["dftc1024"], w=["WOUT"])
                if l == 0:
                    order0 = [("R", c) for c in range(4)] + [("W", c) for c in range(4)]
                    for c in range(4):
                        W, wk = get_w(phase, l, order0, c, w_in, latent)
                        qT = gb(3, BF16)[:, 0:1024]; kT = gb(3, BF16)[:, 1024:2048]
                        proj_fm(qT, lambda k: W[:, k, 0:128], wk, 1024, "G3:qT", 0,
                                (lambda k: W[:, k, 512:640]) if latent else None)
                        proj_fm(kT, lambda k: W[:, k, 128:256], wk, 1024, "G3:kT", 1,
                                (lambda k: W[:, k, 640:768]) if latent else None)
                        chk("r_proj")
                        for d in range(2):
                            act(QD[:, d, :], FT[:, d, :], AF.Exp, r=["ftab", "LGC"], w=["QD"], scale=LGC[:, c, d:d + 1])
                        act(CDt[:], LGC[:, c, :], AF.Exp, r=["LGC"], w=["CDt"], scale=128.0)
                        qTf = gb(4, BF16)[:, 0:1024]; qTb = gb(4, BF16)[:, 1024:2048]
                        for d, dst in ((0, qTf), (1, qTb)):
                            tt("dve", dst.rearrange("p (n i) -> p n i", i=128), qT.rearrange("p (n i) -> p n i", i=128),
                               QD[:, d, :].unsqueeze(1).broadcast_to([128, 8, 128]), ALU.mult, r=["G3:qT", "QD"],
                               w=["G4:qTf%d" % d])
                        Vt = gb(5, BF16)[:, 0:1024].rearrange("p (t f) -> p t f", f=128)
                        Kt = gb(5, BF16)[:, 1024:2048].rearrange("p (t f) -> p t f", f=128)
                        SG = gb(6)[:].rearrange("p (t f) -> p t f", f=128)
                        for t in range(8):
                            tb, tbk = (B4, "B4") if t % 2 == 0 else (B5, "B5")
                            for k in range(8):
                                mm(tb[:, 0:256], UT[:, k, t * 128:(t + 1) * 128], W[:, k, 256:512], k == 0, k == 7,
                                   r=[wk, "UT%d" % t], w=[tbk])
                            cp("dve", Vt[:, t, :], tb[:, 0:128], r=[tbk], w=["G5:Vt"])
                            act(SG[:, t, :], tb[:, 128:256], AF.Silu, r=[tbk], w=["G6:SG"])
                            tr(PTB[:, t * 128:(t + 1) * 128], kT[:, t * 128:(t + 1) * 128], r=["G3:kT"], w=["PTB"])
                        cp("dve", Kt.rearrange("p t f -> p (t f)"), PTB[:, 0:1024], r=["PTB"], w=["G5:Kt"])
                        kTm = make_ktm(kT, "G3:kT")
                        chk("r_vg")
                        Kf = gb(7, BF16)[:, 0:1024].rearrange("p (t f) -> p t f", f=128)
                        Kb = gb(7, BF16)[:, 1024:2048].rearrange("p (t f) -> p t f", f=128)
                        for d, dst in ((0, Kf), (1, Kb)):
                            tt("dve", dst, Kt,
                               KDEC[:, d, 2 * c:2 * c + 2, :].rearrange("p h f -> p (h f)").unsqueeze(1).broadcast_to(
                                   [128, 8, 128]), ALU.mult, r=["G5:Kt", "KDEC"], w=["G7:Kf%d" % d])
                        chk("r_kf")
                        SST = gb(8, BF16)[:, 0:2048].rearrange("p (n d f) -> p n d f", d=2, f=128)
                        RO = gb(9)[:].rearrange("p (t f) -> p t f", f=128)
                        P.op("pool", lambda e: e.memset(gb(8, BF16)[:, 0:2048], 0.0), w=["G8:SST"])
                        nseq = len(seqs)
                        nch = seqs[0][1] // 128
                        skeys = ["S32_%d_%d" % (q, d) for q in range(nseq) for d in range(2)]
                        P.op("pool", lambda e, nseq=nseq: e.memset(S32A[:, 0:nseq], 0.0), w=skeys)
                        if latent:
                            for d in range(2):
                                for hh in range(2):
                                    dma_sp(S32A[hh * 64:(hh + 1) * 64, 0, d, hh * 64:(hh + 1) * 64],
                                           I["st_ret"][d, 2 * c + hh], r=[], w=["S32_0_%d" % d], append=True)
                        chain_banks = [(B5[:, 0:128], "B5"), (B6[:, 0:128], "B6"), (PA[:, 0:128], "PA0"),
                                       (PA[:, 512:640], "PA1"), (PB[:, 0:128], "PB0"), (PB[:, 512:640], "PB1"),
                                       (B4[:, 0:128], "B4"), (B5[:, 128:256], "B5")]
                        for step in range(nch):
                            for q, (s0, T) in enumerate(seqs):
                                t0 = s0 // 128
                                for d in range(2):
                                    n = step if d == 0 else nch - 1 - step
                                    skey = "S32_%d_%d" % (q, d)
                                    for hh in range(2):
                                        rws = slice(hh * 64, (hh + 1) * 64)
                                        cp("act", SST[rws, t0 + n, d, rws], S32A[rws, q, d, rws], r=[skey], w=["G8:SST"])
                                    if (not latent) or step < nch - 1:
                                        bank, bkey = chain_banks[q * 2 + d]
                                        Kd = Kf if d == 0 else Kb
                                        mm(bank, Kd[:, t0 + n, :], Vt[:, t0 + n, :], True, True,
                                           r=["G7:Kf%d" % d, "G5:Vt"], w=[bkey])
                                        stt(S32A[:, q, d, :], S32A[:, q, d, :], CDt[:, d:d + 1], bank, ALU.mult, ALU.add,
                                            r=[skey, "CDt", bkey], w=[skey])
                        chk("r_scan")
                        if not latent:
                            for q, (s0, T) in enumerate(seqs):
                                si = s0 // 256
                                for d in range(2):
                                    for hh in range(2):
                                        dma_sp(O["nstate"][si, d, 2 * c + hh],
                                               S32A[hh * 64:(hh + 1) * 64, q, d, hh * 64:(hh + 1) * 64],
                                               r=["S32_%d_%d" % (q, d)], w=["nstate"], append=True)
                        chk("r_state")
                        sc_b = [(B4[:, 256:512], "B4:s"), (B6[:, 256:512], "B6:s")]
                        o_b = [(B5[:, 256:384], "B5:o"), (PB[:, 256:384], "PB0:o")]
                        at_b = [(gb(10, BF16)[:, 0:256], "G10:a0"), (gb(10, BF16)[:, 256:512], "G10:a1")]
                        chunks = [s0 // 128 + n for (s0, T) in seqs for n in range(T // 128)]

                        def r_S(i):
                            t = chunks[i]
                            tsl = slice(t * 128, (t + 1) * 128)
                            bank, bkey = sc_b[i % 2]
                            for hh in range(2):
                                mm(bank[:, hh * 128:(hh + 1) * 128], kTm[hh][:, tsl], qT[:, tsl], True, True,
                                   r=["G2", "G3:qT"], w=[bkey])

                        def r_R(i):
                            t = chunks[i]
                            tsl = slice(t * 128, (t + 1) * 128)
                            bank, bkey = sc_b[i % 2]
                            AT, atk = at_b[i % 2]
                            ob, obk = o_b[i % 2]
                            tt("dve", AT, bank, DLALL[:, 2 * c:2 * c + 2, :].rearrange("p h i -> p (h i)"),
                               ALU.mult, r=[bkey, "DLALL"], w=[atk])
                            for hh in range(2):
                                rows = slice(hh * 64, (hh + 1) * 64)
                                osl = ob[:, hh * 64:(hh + 1) * 64]
                                mm(osl, AT[:, hh * 128:(hh + 1) * 128], Vt[:, t, rows], True, False, r=[atk, "G5:Vt"], w=[obk])
                                mm(osl, qTf[:, tsl], SST[:, t, 0, rows], False, False, r=["G4:qTf0", "G8:SST"], w=[obk])
                                mm(osl, qTb[:, tsl], SST[:, t, 1, rows], False, True, r=["G4:qTf1", "G8:SST"], w=[obk])
                            cp("dve", RO[:, t, :], ob, r=[obk], w=["G9:RO"])

                        r_S(0)
                        for i in range(len(chunks)):
                            if i + 1 < len(chunks):
                                r_S(i + 1)
                            r_R(i)
                        chk("r_out")
                        ROf = RO.rearrange("p t f -> p (t f)")
                        RO3 = ROf.rearrange("p (a f) -> p a f", f=64)
                        SQ = gb(10)[:]
                        act(SQ, ROf, AF.Square, r=["G9:RO"], w=["G10"])
                        red(sm[:, 16:32], RO3, r=["G9:RO"], w=["smA"])
                        red(sm[:, 32:48], SQ.rearrange("p (a f) -> p a f", f=64), r=["G10"], w=["smB"])
                        ts("dve", sm[:, 16:32], sm[:, 16:32], 1.0 / 64, None, ALU.mult, None, r=["smA"], w=["smA"])
                        tt("dve", sm[:, 48:64], sm[:, 16:32], sm[:, 16:32], ALU.mult, r=["smA"], w=["smC"])
                        stt(sm[:, 32:48], sm[:, 32:48], 1.0 / 64, sm[:, 48:64], ALU.mult, ALU.subtract, r=["smB", "smC"],
                            w=["smB"])
                        ts("dve", sm[:, 32:48], sm[:, 32:48], EPS, None, ALU.add, None, r=["smB"], w=["smB"])
                        rsqrt_pool(sm[:, 32:48], 16, "smB")
                        tt("dve", RO3, RO3, sm[:, 16:32].unsqueeze(2).broadcast_to([128, 16, 64]), ALU.subtract,
                           r=["G9:RO", "smA"], w=["G9:RO"])
                        tt("dve", RO3, RO3, sm[:, 32:48].unsqueeze(2).broadcast_to([128, 16, 64]), ALU.mult,
                           r=["G9:RO", "smB"], w=["G9:RO"])
                        tt("dve", RO, RO, GNG[:, c * 128:(c + 1) * 128].unsqueeze(1).broadcast_to([128, 8, 128]), ALU.mult,
                           r=["G9:RO", "GNG"], w=["G9:RO"])
                        tt("dve", RO, RO, GNB[:, c * 128:(c + 1) * 128].unsqueeze(1).broadcast_to([128, 8, 128]), ALU.add,
                           r=["G9:RO", "GNB"], w=["G9:RO"])
                        ROb = gb(10, BF16)[:, 0:1024].rearrange("p (t f) -> p t f", f=128)
                        tt("dve", ROb, RO, SG, ALU.mult, r=["G9:RO", "G6:SG"], w=["G10"])
                        chk("r_epi")
                        transpose_to_OT(lambda t: ROb[:, t, :], "G10", c)
                        chk("r_done")

                    chk("r_all")
                    kTw = gb(3, BF16)[:, 0:1024]
                    VE = gb(8, BF16)[:, 0:8 * 130].rearrange("p (t f) -> p t f", f=130)
                    CKT = [gb(4, BF16)[:, 0:512], gb(4, BF16)[:, 512:1024]]
                    CVE = gb(4, BF16)[:, 1024:1024 + 4 * 130].rearrange("p (t f) -> p t f", f=130)
                    for c in range(4):
                        W, wk = get_w(phase, l, order0, 4 + c, w_in, latent)
                        if c == 0:
                            proj_fm(kTw, lambda k: W[:, k, 256:384], wk, 1024, "G3:kTw", 0,
                                    (lambda k: W[:, k, 512:640]) if latent else None)
                            P.op("pool", lambda e, VE=VE: e.memset(VE, 1.0), w=["G8:VE"])
                            for t in range(8):
                                tb, tbk = (B4, "B4") if t % 2 == 0 else (B5, "B5")
                                vo = 0 if latent else 128
                                for k in range(8):
                                    if latent:
                                        mm(tb[:, 0:128], UT[:, k, t * 128:(t + 1) * 128], W[:, k, 384:512], k == 0, k == 7,
                                           r=[wk, "UT%d" % t], w=[tbk])
                                    else:
                                        mm(tb[:, 0:256], UT[:, k, t * 128:(t + 1) * 128], W[:, k, 256:512], k == 0, k == 7,
                                           r=[wk, "UT%d" % t], w=[tbk])
                                for kv in range(2):
                                    cp("dve", VE[:, t, kv * 65:kv * 65 + 64], tb[:, vo + kv * 64:vo + (kv + 1) * 64],
                                       r=[tbk], w=["G8:VE"])
                                if not latent:
                                    kvo = gb(5)[:, (t % 2) * 256:(t % 2) * 256 + 256]
                                    cp("act", kvo, tb[:, 0:256], r=[tbk], w=["G5:o%d" % (t % 2)])
                                    sq, tq = t // 2, (t % 2) * 128
                                    dma_sp(O["nwk"][sq, tq:tq + 128, :], kvo[:, 0:128], r=["G5:o%d" % (t % 2)], w=["nwk"],
                                           append=True)
                                    dma_sp(O["nwv"][sq, tq:tq + 128, :], kvo[:, 128:256], r=["G5:o%d" % (t % 2)], w=["nwv"],
                                           append=True)
                            kTm = make_ktm(kTw, "G3:kTw")
                            if latent:
                                ck = gb(5, BF16)[:, 0:512].rearrange("p (t f) -> p t f", f=128)
                                dma_cast(ck, I["cwk"].rearrange("(t p) f -> p t f", p=128), w=["G5:c"])
                                for t in range(4):
                                    tr(PTB[:, t * 128:(t + 1) * 128], ck[:, t, :], r=["G5:c"], w=["PTB"])
                                P.op("pool", lambda e: e.memset(gb(4, BF16)[:, 0:1024], 0.0), w=["G4:CKT"])
                                for hh_ in range(2):
                                    rws = slice(hh_ * 64, (hh_ + 1) * 64)
                                    cp("dve", CKT[hh_][rws, :], PTB[rws, 0:512], r=["PTB"], w=["G4:CKT"])
                                P.op("pool", lambda e, CVE=CVE: e.memset(CVE, 1.0), w=["G4:CVE"])
                                for kv in range(2):
                                    dma_cast(CVE[:, :, kv * 65:kv * 65 + 64],
                                             I["cwv"][:, kv * 64:(kv + 1) * 64].rearrange("(t p) f -> p t f", p=128),
                                             w=["G4:CVE"])
                        qT = gb(6, BF16)[:, 0:1024]
                        proj_fm(qT, lambda k: W[:, k, 0:128], wk, 1024, "G6:qTw", 1,
                                (lambda k: W[:, k, 128:256]) if latent else None)
                        WO = gb(7, BF16)[:, 0:1024].rearrange("p (t f) -> p t f", f=128)
                        for (s0, T) in seqs:
                            nch = T // 128
                            t0 = s0 // 128
                            for n in range(nch):
                                t = t0 + n
                                tsl = slice(t * 128, (t + 1) * 128)
                                for hh in range(2):
                                    rows = slice(hh * 64, (hh + 1) * 64)
                                    subs = []
                                    if latent:
                                        for cc in range(4):
                                            subs.append((CKT[hh][:, cc * 128:(cc + 1) * 128], CVE[:, cc, hh * 65:(hh + 1) * 65],
                                                         None, ["G4:CKT", "G4:CVE"]))
                                        for m in (n - 1, n, n + 1):
                                            if m < 0 or m >= nch:
                                                continue
                                            mask = None
                                            if m == n - 1:
                                                mask = (MPV[:], "mprev")
                                            elif m == n + 1:
                                                mask = (MNX[:], "mnext")
                                            subs.append((kTm[hh][:, (t0 + m) * 128:(t0 + m + 1) * 128],
                                                         VE[:, t0 + m, hh * 65:(hh + 1) * 65], mask, ["G2", "G8:VE"]))
                                    else:
                                        for m in range(nch):
                                            subs.append((kTm[hh][:, (t0 + m) * 128:(t0 + m + 1) * 128],
                                                         VE[:, t0 + m, hh * 65:(hh + 1) * 65], None, ["G2", "G8:VE"]))
                                    ob, obk = (B6, "B6") if t % 2 == 0 else (PB, "PB0")
                                    attn_block(qT[:, tsl], subs, 65, ob[:, hh * 65:(hh + 1) * 65], obk, ["G6:qTw"], 0.125, None)
                                def w_epi(t=t, ob=ob, obk=obk, c=c, WO=WO):
                                    o3 = ob[:, 0:130].rearrange("p (h f) -> p h f", f=65)
                                    tt("dve", sm[:, 0:2], o3[:, :, 64], ES[:, c:8:4], ALU.add, r=[obk, "ES"], w=["sm"])
                                    P.op("dve", lambda e: e.reciprocal(out=sm[:, 2:4], in_=sm[:, 0:2]), r=["sm"], w=["smr"])
                                    tt("dve", WO[:, t, :].rearrange("p (h f) -> p h f", f=64), o3[:, :, 0:64],
                                       sm[:, 2:4].unsqueeze(2).broadcast_to([128, 2, 64]), ALU.mult, r=[obk, "smr"],
                                       w=["G7:WO"])
                                attn_after(w_epi)
                        attn_flush()
                        transpose_to_OT(lambda t: WO[:, t, :], "G7:WO", 4 + c)
                        chk("w_done")
                else:
                    order1 = [("D", h) for h in range(6)] + [("F", 0)]
                    for h in range(6):
                        W, wk = get_w(phase, l, order1, h, w_in, latent)
                        qT = gb(3, BF16)[:, 0:1024]; kT = gb(3, BF16)[:, 1024:2048]
                        proj_fm(qT, lambda k: W[:, k, 0:128], wk, 1024, "G3:qT", 0,
                                (lambda k: W[:, k, 512:640]) if latent else None)
                        proj_fm(kT, lambda k: W[:, k, 256:384], wk, 1024, "G3:kT", 1,
                                (lambda k: W[:, k, 640:768]) if latent else None)
                        if stop == "d_w" and h == 0:
                            for k in range(8):
                                cp("dve", X[:, k, 0:768], W[:, k, :], r=[wk], w=["X%d" % k])
                            raise StopBuild()
                        if stop == "d_ut" and h == 0:
                            for k in range(8):
                                cp("dve", X[:, k, :], UT[:, k, :], r=["UT%d" % t for t in range(8)], w=["X%d" % k])
                            raise StopBuild()
                        VE = gb(5, BF16)[:, 0:8 * 130].rearrange("p (t f) -> p t f", f=130)
                        P.op("pool", lambda e, VE=VE: e.memset(VE, 1.0), w=["G5:VEd"])
                        for t in range(8):
                            tb, tbk = (B4, "B4") if t % 2 == 0 else (B5, "B5")
                            for k in range(8):
                                if latent:
                                    mm(tb[:, 0:128], UT[:, k, t * 128:(t + 1) * 128], W[:, k, 384:512], k == 0, k == 7,
                                       r=[wk, "UT%d" % t], w=[tbk])
                                else:
                                    mm(tb[:, 0:256], UT[:, k, t * 128:(t + 1) * 128], W[:, k, 256:512], k == 0, k == 7,
                                       r=[wk, "UT%d" % t], w=[tbk])
                            cp("dve", VE[:, t, 0:128], tb[:, 0:128] if latent else tb[:, 128:256], r=[tbk], w=["G5:VEd"])
                            if not latent:
                                kvo = gb(6)[:, (t % 2) * 256:(t % 2) * 256 + 256]
                                cp("act", kvo, tb[:, 0:256], r=[tbk], w=["G6:o%d" % (t % 2)])
                                sq, tq = t // 2, (t % 2) * 128
                                dma_sp(O["ndk"][sq, tq:tq + 128, h * 128:(h + 1) * 128], kvo[:, 0:128],
                                       r=["G6:o%d" % (t % 2)], w=["ndk"], append=True)
                                dma_sp(O["ndv"][sq, tq:tq + 128, h * 128:(h + 1) * 128], kvo[:, 128:256],
                                       r=["G6:o%d" % (t % 2)], w=["ndv"], append=True)
                        kTm = make_ktm(kT, "G3:kT")
                        chk("d_kv")
                        CKT = [gb(4, BF16)[:, 0:512], gb(4, BF16)[:, 512:1024]]
                        CVE = gb(4, BF16)[:, 1024:1024 + 4 * 130].rearrange("p (t f) -> p t f", f=130)
                        if latent:
                            ck = gb(6, BF16)[:, 0:512].rearrange("p (t f) -> p t f", f=128)
                            dma_cast(ck, I["cdk"][:, h * 128:(h + 1) * 128].rearrange("(t p) f -> p t f", p=128), w=["G6:c"])
                            for t in range(4):
                                tr(PTB[:, t * 128:(t + 1) * 128], ck[:, t, :], r=["G6:c"], w=["PTB"])
                            P.op("pool", lambda e: e.memset(gb(4, BF16)[:, 0:1024], 0.0), w=["G4:CKT"])
                            for hh_ in range(2):
                                rws = slice(hh_ * 64, (hh_ + 1) * 64)
                                cp("dve", CKT[hh_][rws, :], PTB[rws, 0:512], r=["PTB"], w=["G4:CKT"])
                            P.op("pool", lambda e, CVE=CVE: e.memset(CVE, 1.0), w=["G4:CVE"])
                            dma_cast(CVE[:, :, 0:128], I["cdv"][:, h * 128:(h + 1) * 128].rearrange("(t p) f -> p t f", p=128),
                                     w=["G4:CVE"])
                        AO = gb(7)[:].rearrange("p (t f) -> p t f", f=128)
                        for (s0, T) in seqs:
                            nch = T // 128
                            t0 = s0 // 128
                            for n in range(nch):
                                t = t0 + n
                                tsl = slice(t * 128, (t + 1) * 128)
                                for m in range(2):
                                    rows = slice(m * 64, (m + 1) * 64)
                                    subs = []
                                    for mm_ in range(nch):
                                        subs.append((kTm[m][:, (t0 + mm_) * 128:(t0 + mm_ + 1) * 128], VE[:, t0 + mm_, 0:129],
                                                     None, ["G2", "G5:VEd"]))
                                    if latent:
                                        for cc in range(4):
                                            subs.append((CKT[m][:, cc * 128:(cc + 1) * 128], CVE[:, cc, 0:129], None,
                                                         ["G4:CKT", "G4:CVE"]))
                                    ob, obk = (B6, "B6") if t % 2 == 0 else (PB, "PB0")
                                    attn_block(qT[:, tsl], subs, 129, ob[:, m * 129:(m + 1) * 129], obk, ["G3:qT"], 0.125, None)
                                def d_epi(t=t, ob=ob, obk=obk, AO=AO):
                                    o3 = ob[:, 0:258].rearrange("p (h f) -> p h f", f=129)
                                    P.op("dve", lambda e, o3=o3: e.reciprocal(out=sm[:, 0:2], in_=o3[:, :, 128]), r=[obk],
                                         w=["sm"])
                                    tt("dve", sm[:, 1:2], sm[:, 1:2], LAM[:, 2:3], ALU.mult, r=["sm", "LAM"], w=["sm"])
                                    a1 = gb(8)[:, 0:128]
                                    ts("dve", a1, o3[:, 0, 0:128], sm[:, 0:1], None, ALU.mult, None, r=[obk, "sm"], w=["G8"])
                                    stt(AO[:, t, :], o3[:, 1, 0:128], sm[:, 1:2], a1, ALU.mult, ALU.add,
                                        r=[obk, "sm", "G8"], w=["G7:AO"])
                                attn_after(d_epi)
                        attn_flush()
                        AOf = AO.rearrange("p t f -> p (t f)")
                        SQ = gb(8)[:]
                        act(SQ, AOf, AF.Square, r=["G7:AO"], w=["G8"])
                        red(sm[:, 16:24], SQ.rearrange("p (t f) -> p t f", f=128), r=["G8"], w=["smB"])
                        ts("dve", sm[:, 16:24], sm[:, 16:24], 1.0 / 128, EPS, ALU.mult, ALU.add, r=["smB"], w=["smB"])
                        rsqrt_pool(sm[:, 16:24], 8, "smB")
                        tt("dve", AO, AO, sm[:, 16:24].unsqueeze(2).broadcast_to([128, 8, 128]), ALU.mult, r=["G7:AO", "smB"],
                           w=["G7:AO"])
                        AOb = gb(8, BF16)[:, 0:1024].rearrange("p (t f) -> p t f", f=128)
                        tt("dve", AOb, AO, SUB[:].unsqueeze(1).broadcast_to([128, 8, 128]), ALU.mult, r=["G7:AO", "SUB"],
                           w=["G8"])
                        transpose_to_OT(lambda t: AOb[:, t, :], "G8", h)
                        chk("d_done")
                    W, wk = get_w(phase, l, order1, 6, w_in, latent)
                    ZT = gb(3, BF16)[:].rearrange("p (c t) -> p c t", c=2)
                    for cc in range(2):
                        proj_fm(ZT[:, cc, :], lambda k, cc=cc: W[:, k, cc * 128:(cc + 1) * 128], wk, 1024, "G3:ZT%d" % cc, cc, None)
                    Asb = gb(4, BF16)[:].rearrange("p (t f) -> p t f", f=256)
                    Bsb = gb(5, BF16)[:].rearrange("p (t f) -> p t f", f=256)
                    for t in range(8):
                        for cc in range(2):
                            mm(B4[:, cc * 128:(cc + 1) * 128], ZT[:, cc, t * 128:(t + 1) * 128], BDC[:], True, True,
                               r=["G3:ZT%d" % cc, "bdc"], w=["B4"])
                            mm(B4[:, 256 + cc * 128:256 + (cc + 1) * 128], ZT[:, cc, t * 128:(t + 1) * 128], BDS[:], True, True,
                               r=["G3:ZT%d" % cc, "bds"], w=["B4"])
                        cp("dve", Asb[:, t, :], B4[:, 0:256], r=["B4"], w=["G4:Asb"])
                        cp("act", Bsb[:, t, :], B4[:, 256:512], r=["B4"], w=["G5:Bsb"])
                    Yb = gb(6, BF16)[:].rearrange("p (t f) -> p t f", f=256)
                    if latent:
                        YA = gb(7)[:, :].rearrange("p (t f) -> p t f", f=256)
                        YA2 = gb(8)[:, :].rearrange("p (t f) -> p t f", f=256)
                        ST = [(gb(0, BF16), "G0"), (gb(1, BF16), "G1"), (gb(9, BF16), "G9"), (gb(10, BF16), "G10")]
                        for j, (stb, stk) in enumerate(ST):
                            dma_sp(stb[:, 0:2048].rearrange("p (t n) -> p t n", t=2), I["dfts1024"][:, 2 * j:2 * j + 2, :],
                                   w=[stk])
                        for tp in range(8):
                            for t in range(8):
                                mm(B5[:, 0:256], WOUT[:, t, tp * 128:(tp + 1) * 128], Asb[:, t, :], t == 0, t == 7,
                                   r=["WOUT", "G4:Asb"], w=["B5"])
                            ya = YA[:, tp, :] if tp < 4 else YA2[:, tp - 4, :]
                            cp("act", ya, B5[:, 0:256], r=["B5"], w=["G7" if tp < 4 else "G8"])
                        load_w_out(l, w_out)
                        for tp in range(8):
                            for t in range(8):
                                stb, stk = ST[t // 2]
                                mm(B5[:, 0:256], stb[:, (t % 2) * 1024 + tp * 128:(t % 2) * 1024 + (tp + 1) * 128], Bsb[:, t, :],
                                   t == 0, t == 7, r=[stk, "G5:Bsb"], w=["B5"])
                            ya = YA[:, tp, :] if tp < 4 else YA2[:, tp - 4, :]
                            tt("dve", Yb[:, tp, :], ya, B5[:, 0:256], ALU.add, r=["B5", "G7" if tp < 4 else "G8"], w=["G6:Yb"])
                    else:
                        for (s0, T) in seqs:
                            t0 = s0 // 128
                            for tp in range(2):
                                for t in range(2):
                                    mm(B5[:, 0:256], DF256[:, 0, t, tp * 128:(tp + 1) * 128], Asb[:, t0 + t, :], t == 0, False,
                                       r=["DF256", "G4:Asb"], w=["B5"])
                                for t in range(2):
                                    mm(B5[:, 0:256], DF256[:, 1, t, tp * 128:(tp + 1) * 128], Bsb[:, t0 + t, :], False, t == 1,
                                       r=["DF256", "G5:Bsb"], w=["B5"])
                                cp("dve", Yb[:, t0 + tp, :], B5[:, 0:256], r=["B5"], w=["G6:Yb"])
                    for cc in range(2):
                        transpose_to_OT(lambda t, cc=cc: Yb[:, t, cc * 128:(cc + 1) * 128], "G6:Yb", 6 + cc)

                chk("ot%d" % l)
                for t in range(8):
                    for hf in range(2):
                        for k in range(8):
                            mm(PA[:, hf * 512:(hf + 1) * 512], OT[:, k, t * 128:(t + 1) * 128], WOUT[:, k, hf * 512:(hf + 1) * 512],
                               k == 0, k == 7, r=["OT%d" % k, "WOUT"], w=["PA%d" % hf])
                    tmp = gb(0)
                    for hf in range(2):
                        tt("dve", tmp[:, hf * 512:(hf + 1) * 512], PA[:, hf * 512:(hf + 1) * 512], BC[0][:, hf * 512:(hf + 1) * 512],
                           ALU.mult, r=["PA%d" % hf, "BC0"], w=["G0"])
                    stt(X[:, t, :], X[:, t, :], ALPHA, tmp, ALU.mult, ALU.add, r=["X%d" % t, "G0"], w=["X%d" % t])
                if phase == 0:
                    mod_load(l, 1, 0, 0)
                    mod_load(l, 1, 1, 1)
                post_ln_all()
                chk("mix%d" % l)
                ffn_state = {}
                ffn_loaded = {}

                def ffn_load(s_, l=l):
                    if s_ in ffn_loaded:
                        return
                    W, wk = next_wg()
                    load_cols(W[:, :, 0:256], I["w_gate"][l], s_ * 256, 256, wk)
                    load_cols(W[:, :, 256:512], I["w_up"][l], s_ * 256, 256, wk)
                    wd = WDr[s_ % 2]; wdk = "WD%d" % (s_ % 2)
                    wds = WDS[s_ % 2]; wdsk = "WDS%d" % (s_ % 2)
                    dma_cast(wd[:], I["w_down"][l][s_ * 256:(s_ + 1) * 256, :].rearrange("(j p) n -> p j n", p=128),
                             w=[wdk])
                    ffn_loaded[s_] = (W, wk, wd, wdk, wds, wdsk)

                if phase == 1:
                    ffn_load(0)
                    ffn_load(1)
                compute_mod(l, 1, phase)
                if phase == 0:
                    ffn_load(0)
                    ffn_load(1)
                chk("mod2")
                ln_to_UT(1, phase, l)
                chk("ut2")
                def ffn_GU(s_, hf):
                    if hf == 0:
                        ffn_load(s_)
                        W, wk, wd, wdk, wds, wdsk = ffn_loaded[s_]
                        tt("dve", wds[:], wd[:], BC[0][:].unsqueeze(1).broadcast_to([128, 2, 1024]), ALU.mult,
                           r=[wdk, "BC0"], w=[wdsk])
                        ffn_state[s_] = (W, wk, wds, wdsk)
                    W, wk, wds, wdsk = ffn_state[s_]
                    ukeys = ["UT%d" % t for t in range(hf * 4, hf * 4 + 4)]
                    hT = gb(4 + hf, BF16)[:, 0:1024].rearrange("p (j t) -> p j t", j=2)
                    hk = "G%d" % (4 + hf)
                    for j in range(2):
                        pp = PA if j == 0 else PB
                        pk = "PA" if j == 0 else "PB"
                        for k in range(8):
                            mm(pp[:, 0:512], W[:, k, j * 128:(j + 1) * 128], UT[:, k, hf * 512:(hf + 1) * 512], k == 0, k == 7,
                               r=[wk] + ukeys, w=[pk + "0"])
                        for k in range(8):
                            mm(pp[:, 512:1024], W[:, k, 256 + j * 128:256 + (j + 1) * 128], UT[:, k, hf * 512:(hf + 1) * 512],
                               k == 0, k == 7, r=[wk] + ukeys, w=[pk + "1"])
                        sg = gb(3)[:, j * 512:(j + 1) * 512]
                        act(sg, pp[:, 0:512], AF.Silu, r=[pk + "0"], w=["G3:s%d" % j])
                        tt("dve", hT[:, j, :], sg, pp[:, 512:1024], ALU.mult, r=["G3:s%d" % j, pk + "1"], w=[hk])

                def ffn_D(s_, hf):
                    W, wk, wds, wdsk = ffn_state[s_]
                    hT = gb(4 + hf, BF16)[:, 0:1024].rearrange("p (j t) -> p j t", j=2)
                    hk = "G%d" % (4 + hf)
                    for tq in range(4):
                        t = hf * 4 + tq
                        for oh in range(2):
                            bank, bkey = ((B4[:, 0:512], "B4"), (B5[:, 0:512], "B5"),
                                          (B6[:, 0:512], "B6"))[(tq * 2 + oh) % 3]
                            for j in range(2):
                                mm(bank[:, 0:512], hT[:, j, tq * 128:(tq + 1) * 128], wds[:, j, oh * 512:(oh + 1) * 512],
                                   j == 0, j == 1, r=[hk, wdsk], w=[bkey])
                            if s_ == 0:
                                stt(X[:, t, oh * 512:(oh + 1) * 512], X[:, t, oh * 512:(oh + 1) * 512], ALPHA, bank[:, 0:512],
                                    ALU.mult, ALU.add, r=["X%d" % t, bkey], w=["X%d" % t])
                            else:
                                tt("dve", X[:, t, oh * 512:(oh + 1) * 512], X[:, t, oh * 512:(oh + 1) * 512], bank[:, 0:512],
                                   ALU.add, r=["X%d" % t, bkey], w=["X%d" % t])

                prev_u = None
                for s_ in range(11):
                    for hf in range(2):
                        ffn_GU(s_, hf)
                        if prev_u is not None:
                            ffn_D(*prev_u)
                        prev_u = (s_, hf)
                ffn_D(*prev_u)
                if phase == 0 and l == 0:
                    mod_load(1, 0, 0, 0)
                    mod_load(1, 0, 1, 1)
                post_ln_all()
                chk("l%d" % l)
            chk("p%d" % phase)
            for t in range(8):
                dma_sp(O["y"][tok0 + t * 128: tok0 + (t + 1) * 128, :], X[:, t, :], r=["X%d" % t], w=["y"])
          except StopBuild:
            for t in range(8):
                dma_sp(O["y"][tok0 + t * 128: tok0 + (t + 1) * 128, :], X[:, t, :], r=["X%d" % t], w=["y"])
            break
        P.emit()
    nc._marks = P.marks
    return nc


def make_in_maps(inp):
    consts = host_constants()
    maps = []
    f = lambda a: np.ascontiguousarray(np.asarray(a, dtype=np.float32))
    for i in range(NCORES):
        b = i // 4
        m = {}
        m["xin"] = np.concatenate([f(inp["x_prompt"][4 * i:4 * i + 4]).reshape(1024, 1024), f(inp["x_sample"][b])], 0)
        m["cond"] = np.stack([f(inp["c_ctx"]), f(inp["c"][b])], 0)
        m["st_ret"] = f(inp["state_ret"][b, 0])
        m["cwk"] = f(inp["cache_win_k"][b, 0]).reshape(512, 128)
        m["cwv"] = f(inp["cache_win_v"][b, 0]).reshape(512, 128)
        m["cdk"] = f(inp["cache_diff_k"][b, 0]).reshape(512, 768)
        m["cdv"] = f(inp["cache_diff_v"][b, 0]).reshape(512, 768)
        m["w_mod"] = f(inp["w_mod"]); m["b_mod"] = f(inp["b_mod"])
        m["ln_g"] = f(inp["ln_g"]); m["ln_b"] = f(inp["ln_b"])
        m["w_in_ab"] = f(inp["w_in_ab"][0]); m["w_out_ab"] = f(inp["w_out_ab"][0])
        m["lgam"] = f(inp["ret_log_gamma"][0]).reshape(16)
        m["gn_g"] = f(inp["ret_gn_g"][0]); m["gn_b"] = f(inp["ret_gn_b"][0])
        m["sink"] = f(inp["win_sink"][0])
        m["w_in_cd"] = f(inp["w_in_cd"][0]); m["w_out_cd"] = f(inp["w_out_cd"][0])
        m["dlam"] = f(inp["diff_lambda"][0]); m["subln"] = f(inp["diff_subln_g"][0])
        m["w_gate"] = f(inp["w_gate"]); m["w_up"] = f(inp["w_up"]); m["w_down"] = f(inp["w_down"])
        m.update(consts)
        maps.append(m)
    return maps


_NC_CACHE = {}


def kernel(**inp):
    if "nc" not in _NC_CACHE:
        _NC_CACHE["nc"] = build_nc()
    nc = _NC_CACHE["nc"]
    maps = make_in_maps(inp)
    res = run_bass_kernel_spmd(nc, maps, core_ids=list(range(NCORES)))
    R = res.results
    y_prompt = np.concatenate([R[i]["y"][0:1024].reshape(4, 256, 1024) for i in range(NCORES)], 0)
    y_sample = np.stack([R[0]["y"][1024:2048], R[4]["y"][1024:2048]], 0)
    nstate = np.concatenate([R[i]["nstate"] for i in range(NCORES)], 0).reshape(32, 1, 2, 8, 64, 64)
    nwk = np.concatenate([R[i]["nwk"] for i in range(NCORES)], 0).reshape(32, 1, 256, 2, 64)
    nwv = np.concatenate([R[i]["nwv"] for i in range(NCORES)], 0).reshape(32, 1, 256, 2, 64)
    ndk = np.concatenate([R[i]["ndk"] for i in range(NCORES)], 0).reshape(32, 1, 256, 6, 2, 64)
    ndv = np.concatenate([R[i]["ndv"] for i in range(NCORES)], 0).reshape(32, 1, 256, 6, 128)
    outs = (y_prompt, y_sample, nstate, nwk, nwv, ndk, ndv)
    return tuple(np.ascontiguousarray(o.astype(np.float32)) for o in outs)
```
